# Optimizing a Trainium2 kernel written in Bass

```python
import math
import jax, jax.numpy as jnp
from jax import lax
import numpy as np

D_MODEL = 1024
BATCH = 8
SEQ = 4096
DEPTH = 1

HEAD_DIM = 64
A_Q_HEADS = 8
A_KV_HEADS = 2
A_GROUP = A_Q_HEADS // A_KV_HEADS
A_HALF_WINDOW = 128
A_BLOCK = 128
B_PATTERNS = ((128, 1), (512, 4), (2048, 16))
B_N_GROUPS = len(B_PATTERNS)
B_HEADS_PER_GROUP = 4
B_BLOCK = 64
ROPE_THETA = 10000.0
D_FF = 3 * D_MODEL
CONV_WIDTH = 3
RMS_EPS = 1e-6
NEG_INF = -1e30

A_Q_COLS = A_Q_HEADS * HEAD_DIM
A_KV_COLS = A_KV_HEADS * HEAD_DIM
A_COLS = A_Q_COLS + 2 * A_KV_COLS
B_PROJ_COLS = B_N_GROUPS * B_HEADS_PER_GROUP * HEAD_DIM
IN_COLS = A_COLS + 3 * B_PROJ_COLS
A_OUT = A_Q_COLS
B_OUT = B_HEADS_PER_GROUP * HEAD_DIM

kernel_name = "hybrid_gated_window_dilated_attention_convffn"


def rms_norm(x, gain):
    xf = x.astype(jnp.float32)
    y = xf * lax.rsqrt(jnp.mean(xf * xf, axis=-1, keepdims=True) + RMS_EPS)
    return (y * gain.astype(jnp.float32)).astype(x.dtype)


def rope(t, seq_len):
    dh = t.shape[-1]
    half = dh // 2
    inv = ROPE_THETA ** (-jnp.arange(half, dtype=jnp.float32) / half)
    ang = jnp.arange(seq_len, dtype=jnp.float32)[:, None] * inv[None, :]
    bshape = (1, seq_len) + (1,) * (t.ndim - 3) + (half,)
    cos = jnp.cos(ang).reshape(bshape)
    sin = jnp.sin(ang).reshape(bshape)
    tf = t.astype(jnp.float32)
    t1, t2 = tf[..., :half], tf[..., half:]
    return jnp.concatenate([t1 * cos - t2 * sin, t2 * cos + t1 * sin], axis=-1).astype(t.dtype)


def banded_attention(q, k, v, half_window, block, sink=None):
    b, L, hkv, g, dh = q.shape
    nb = -(-L // block)
    lp = nb * block
    qp = jnp.pad(q, ((0, 0), (0, lp - L), (0, 0), (0, 0), (0, 0))).reshape(b, nb, block, hkv, g, dh)

    def key_windows(t):
        tp = jnp.pad(t, ((0, 0), (block, lp - L + block), (0, 0), (0, 0))).reshape(b, nb + 2, block, hkv, dh)
        return jnp.concatenate([tp[:, :-2], tp[:, 1:-1], tp[:, 2:]], axis=2)

    kw = key_windows(k)
    vw = key_windows(v)
    qi = jnp.arange(nb)[:, None] * block + jnp.arange(block)[None, :]
    kj = (jnp.arange(nb)[:, None] - 1) * block + jnp.arange(3 * block)[None, :]
    kjb = kj[:, None, :]
    mask = (jnp.abs(kjb - qi[:, :, None]) <= half_window) & (kjb >= 0) & (kjb < L)

    s = jnp.einsum('bnqhgd,bnkhd->bnhgqk', qp.astype(jnp.float32), kw.astype(jnp.float32)) * (dh ** -0.5)
    s = jnp.where(mask[None, :, None, None], s, NEG_INF)
    m = jnp.max(s, axis=-1)
    if sink is not None:
        sk = sink.astype(jnp.float32)[None, None, :, :, None]
        m = jnp.maximum(m, sk)
    p = jnp.exp(s - m[..., None])
    denom = jnp.sum(p, axis=-1)
    if sink is not None:
        denom = denom + jnp.exp(sk - m)
    o = jnp.einsum('bnhgqk,bnkhd->bnqhgd', p, vw.astype(jnp.float32))
    o = o / jnp.moveaxis(denom, -1, 2)[..., None]
    o = o.reshape(b, lp, hkv, g, dh)[:, :L].astype(q.dtype)
    lse = jnp.moveaxis(m + jnp.log(denom), -1, 2).reshape(b, lp, hkv, g)[:, :L]
    return o, lse


def stride_gather(t, d):
    b, s = t.shape[0], t.shape[1]
    t = t.reshape((b, s // d, d) + t.shape[2:])
    t = jnp.moveaxis(t, 2, 1)
    return t.reshape((b * d, s // d) + t.shape[3:])


def stride_scatter(t, d, b):
    sd = t.shape[1]
    t = t.reshape((b, d, sd) + t.shape[2:])
    t = jnp.moveaxis(t, 1, 2)
    return t.reshape((b, sd * d) + t.shape[3:])


def depthwise_conv_centered(u, w, bias):
    up = jnp.pad(u, ((0, 0), (1, 1), (0, 0)))
    return up[:, :-2] * w[0] + up[:, 1:-1] * w[1] + up[:, 2:] * w[2] + bias


def token_mixer(h, w_in, sink, w_branch_a, w_branch_b, w_gate, b_gate, w_out):
    b, s, _ = h.shape
    proj = h @ w_in
    qa = proj[..., :A_Q_COLS].reshape(b, s, A_KV_HEADS, A_GROUP, HEAD_DIM)
    ka = proj[..., A_Q_COLS:A_Q_COLS + A_KV_COLS].reshape(b, s, A_KV_HEADS, HEAD_DIM)
    va = proj[..., A_Q_COLS + A_KV_COLS:A_COLS].reshape(b, s, A_KV_HEADS, HEAD_DIM)
    qa = rope(qa, s)
    ka = rope(ka, s)
    ya, _ = banded_attention(qa, ka, va, A_HALF_WINDOW, A_BLOCK,
                             sink=sink.reshape(A_KV_HEADS, A_GROUP))
    ya = ya.reshape(b, s, A_OUT)
    off = A_COLS
    qb = proj[..., off:off + B_PROJ_COLS].reshape(b, s, B_N_GROUPS, B_HEADS_PER_GROUP, HEAD_DIM)
    kb = proj[..., off + B_PROJ_COLS:off + 2 * B_PROJ_COLS].reshape(b, s, B_N_GROUPS, B_HEADS_PER_GROUP, HEAD_DIM)
    vb = proj[..., off + 2 * B_PROJ_COLS:off + 3 * B_PROJ_COLS].reshape(b, s, B_N_GROUPS, B_HEADS_PER_GROUP, HEAD_DIM)
    qb = rope(qb, s)
    kb = rope(kb, s)
    outs, lses = [], []
    for gi, (window, dil) in enumerate(B_PATTERNS):
        qg = stride_gather(qb[:, :, gi], dil)[:, :, :, None, :]
        kg = stride_gather(kb[:, :, gi], dil)
        vg = stride_gather(vb[:, :, gi], dil)
        og, lg = banded_attention(qg, kg, vg, window // (2 * dil), B_BLOCK)
        outs.append(stride_scatter(og[:, :, :, 0], dil, b))
        lses.append(stride_scatter(lg[:, :, :, 0], dil, b))
    outs = jnp.stack(outs, axis=2)
    wts = jax.nn.softmax(jnp.stack(lses, axis=2), axis=2)
    yb = jnp.sum(wts[..., None] * outs.astype(jnp.float32), axis=2).astype(h.dtype).reshape(b, s, B_OUT)
    gates = jax.nn.sigmoid((h @ w_gate + b_gate).astype(jnp.float32)).astype(h.dtype)
    ga, gb = gates[..., :D_MODEL], gates[..., D_MODEL:]
    merged = ga * (ya @ w_branch_a) + gb * (yb @ w_branch_b)
    return merged @ w_out


def conv_ffn(h, w_up, conv_w, conv_b, w_down):
    u = depthwise_conv_centered(h @ w_up, conv_w, conv_b)
    gate, up = u[..., :D_FF], u[..., D_FF:]
    return (jax.nn.gelu(gate, approximate=True) * up) @ w_down


def setup_inputs(seed: int = 0) -> dict:
    key = jax.random.key(seed)
    ks = jax.random.split(key, 20)
    f32 = jnp.float32

    def nrm(k, shape, scale):
        return jax.random.normal(k, shape, f32) * scale

    def gain(k):
        return 1.0 + 0.05 * jax.random.normal(k, (DEPTH, D_MODEL), f32)

    return {
        "x": jax.random.normal(ks[0], (BATCH, SEQ, D_MODEL), f32),
        "norm_mix_pre": gain(ks[1]),
        "w_in": nrm(ks[2], (DEPTH, D_MODEL, IN_COLS), D_MODEL ** -0.5),
        "sink": nrm(ks[3], (DEPTH, A_Q_HEADS), 1.0),
        "w_branch_a": nrm(ks[4], (DEPTH, A_OUT, D_MODEL), A_OUT ** -0.5),
        "w_branch_b": nrm(ks[5], (DEPTH, B_OUT, D_MODEL), B_OUT ** -0.5),
        "w_gate": nrm(ks[6], (DEPTH, D_MODEL, 2 * D_MODEL), D_MODEL ** -0.5),
        "b_gate": nrm(ks[7], (DEPTH, 2 * D_MODEL), 0.02),
        "w_out": nrm(ks[8], (DEPTH, D_MODEL, D_MODEL), D_MODEL ** -0.5),
        "norm_mix_post": gain(ks[9]),
        "norm_ffn_pre": gain(ks[10]),
        "w_up": nrm(ks[11], (DEPTH, D_MODEL, 2 * D_FF), D_MODEL ** -0.5),
        "conv_w": nrm(ks[12], (DEPTH, CONV_WIDTH, 2 * D_FF), CONV_WIDTH ** -0.5),
        "conv_b": nrm(ks[13], (DEPTH, 2 * D_FF), 0.02),
        "w_down": nrm(ks[14], (DEPTH, D_FF, D_MODEL), D_FF ** -0.5),
        "norm_ffn_post": gain(ks[15]),
    }


def reference(x, norm_mix_pre, w_in, sink, w_branch_a, w_branch_b, w_gate, b_gate, w_out,
              norm_mix_post, norm_ffn_pre, w_up, conv_w, conv_b, w_down, norm_ffn_post):
    for layer in range(DEPTH):
        h = rms_norm(x, norm_mix_pre[layer])
        mix = token_mixer(h, w_in[layer], sink[layer], w_branch_a[layer], w_branch_b[layer],
                          w_gate[layer], b_gate[layer], w_out[layer])
        x = x + rms_norm(mix, norm_mix_post[layer])
        h = rms_norm(x, norm_ffn_pre[layer])
        f = conv_ffn(h, w_up[layer], conv_w[layer], conv_b[layer], w_down[layer])
        x = x + rms_norm(f, norm_ffn_post[layer])
    return x
```

```python
import contextlib
import numpy as np
import concourse.bass as bass
import concourse.mybir as mybir
from concourse.bass_utils import run_bass_kernel_spmd

F32 = mybir.dt.float32
BF16 = mybir.dt.bfloat16
ALU = mybir.AluOpType
AF = mybir.ActivationFunctionType

S = 4096
D = 1024
DFF = 3072
EPS = 1e-6
ENG = ["sync", "scalar", "vector", "gpsimd", "tensor"]
ARENA_F32 = 48384


class Prog:
    def __init__(self, nc, stack):
        self.nc = nc
        self.stack = stack
        self.sems = []
        self.streams = {e: [] for e in ENG}
        self.esem = {}
        self.ecount = {}
        for e in ENG:
            if e != "sync":
                self.esem[e] = self.newsem(e)
                self.ecount[e] = 0
        self.res = {}
        self.waited = {e: {} for e in ENG}
        self.dsem = {}
        self.cur = {}
        self.dma_sems = set()

    def newsem(self, name):
        h = self.stack.enter_context(self.nc.semaphore(f"s{len(self.sems)}_{name}"))
        self.sems.append(h)
        return len(self.sems) - 1

    def _deps(self, eng, reads, writes):
        w = []
        for r in reads:
            st = self.res.get(r)
            if st is not None and st["w"] is not None:
                w.append(st["w"])
        for x in writes:
            st = self.res.get(x)
            if st is not None:
                if st["w"] is not None:
                    w.append(st["w"])
                w.extend(st["r"].items())
        best = {}
        for si, val in w:
            if si in self.dma_sems:
                val = self.cur[si]
            if best.get(si, 0) < val:
                best[si] = val
        out = []
        wd = self.waited[eng]
        for si, val in best.items():
            if wd.get(si, 0) < val:
                wd[si] = val
                out.append((si, val))
        return out

    def _update(self, t, reads, writes):
        for r in reads:
            st = self.res.setdefault(r, {"w": None, "r": {}})
            if st["r"].get(t[0], 0) < t[1]:
                st["r"][t[0]] = t[1]
        for x in writes:
            self.res[x] = {"w": t, "r": {}}

    def op(self, eng, fn, reads=(), writes=()):
        waits = self._deps(eng, reads, writes)
        if self.ecount[eng] >= 30000:
            self.esem[eng] = self.newsem(eng)
            self.ecount[eng] = 0
        self.ecount[eng] += 1
        si, val = self.esem[eng], self.ecount[eng]
        self.cur[si] = val
        sems = self.sems

        def emit(e):
            for wi, wv in waits:
                e.wait_ge(sems[wi], wv)
            fn(e).then_inc(sems[si], 1)

        self.streams[eng].append(emit)
        t = (si, val)
        self._update(t, reads, writes)
        return t

    def dma(self, eng, key, fn, reads=(), writes=()):
        waits = self._deps(eng, reads, writes)
        ent = self.dsem.get(key)
        if ent is None or ent[1] >= 30000:
            ent = [self.newsem("d"), 0]
            self.dsem[key] = ent
            self.dma_sems.add(ent[0])
        ent[1] += 16
        si, val = ent[0], ent[1]
        self.cur[si] = val
        sems = self.sems

        def emit(e):
            for wi, wv in waits:
                e.wait_ge(sems[wi], wv)
            fn(e).then_inc(sems[si], 16)

        self.streams[eng].append(emit)
        t = (si, val)
        self._update(t, reads, writes)
        return t

    def barrier(self, engines=ENG):
        sems = self.sems
        for eng in engines:
            ws = []
            wd = self.waited[eng]
            for si, val in self.cur.items():
                if wd.get(si, 0) < val:
                    wd[si] = val
                    ws.append((si, val))

            def emit(e, ws=ws):
                for wi, wv in ws:
                    e.wait_ge(sems[wi], wv)

            self.streams[eng].append(emit)
        self.res = {}


class Arena:
    def __init__(self, ap):
        self.ap = ap
        self.off = 0

    def reset(self, off=0):
        self.off = off

    def _shape(self, a, shape):
        if len(shape) == 2:
            return a
        if len(shape) == 3:
            return a.rearrange("p (a b) -> p a b", a=shape[1])
        if len(shape) == 4:
            return a.rearrange("p (a b c) -> p a b c", a=shape[1], b=shape[2])
        raise ValueError(shape)

    def f32(self, shape):
        n = int(np.prod(shape[1:]))
        assert self.off + n <= ARENA_F32, (self.off, n)
        a = self.ap[:, self.off:self.off + n]
        self.off += n
        return self._shape(a, shape)

    def bf16_at(self, off, shape):
        save = self.off
        self.off = off
        a = self.bf16(shape)
        self.off = save
        return a

    def bf16(self, shape):
        n = int(np.prod(shape[1:]))
        nf = (n + 1) // 2
        assert self.off + nf <= ARENA_F32, (self.off, nf)
        a = self.ap[:, self.off:self.off + nf].bitcast(BF16)[:, 0:n]
        self.off += nf
        return self._shape(a, shape)


def build(upto=99, debug=False):
    nc = bass.Bass("TRN2", target_bir_lowering=False)
    dt = nc.dram_tensor

    def inp(name, shape):
        return dt(name, shape, F32, kind="ExternalInput").ap()

    x = inp("x", [S, D])
    w_in = inp("w_in", [D, 3072])
    w_gate = inp("w_gate", [D, 2048])
    b_gate = inp("b_gate", [2048])
    w_out = inp("w_out", [D, D])
    w_a = inp("w_branch_a", [512, D])
    w_b = inp("w_branch_b", [256, D])
    w_up = inp("w_up", [D, 2 * DFF])
    conv_w = inp("conv_w", [3, 2 * DFF])
    conv_b = inp("conv_b", [2 * DFF])
    w_down = inp("w_down", [DFF, D])
    n_pre = inp("norm_mix_pre", [D])
    n_post = inp("norm_mix_post", [D])
    n_fpre = inp("norm_ffn_pre", [D])
    n_fpost = inp("norm_ffn_post", [D])
    sink = inp("sink", [8])
    c_ident = inp("c_ident", [128, 128])
    c_masks = inp("c_masks", [128, 4 * 128])
    c_cos = inp("c_cos", [128, S])
    c_sin = inp("c_sin", [128, S])
    c_perm = inp("c_perm", [128, 128])
    y = dt("y", [S, D], F32, kind="ExternalOutput").ap()

    qk_d = dt("qk_d", [18, 128, S], BF16, kind="Internal").ap()
    v_d = dt("v_d", [S, 896], BF16, kind="Internal").ap()
    hT_d = dt("hT_d", [128, 8, S], BF16, kind="Internal").ap()
    x1_d = dt("x1_d", [S, D], F32, kind="Internal").ap()
    gT_d = dt("gT_d", [24, 128, S], BF16, kind="Internal").ap()
    dbg = {}
    if debug:
        dbg["qk"] = dt("dbg_qk", [18, 128, S], BF16, kind="ExternalOutput").ap()
        dbg["v"] = dt("dbg_v", [S, 896], BF16, kind="ExternalOutput").ap()
        dbg["yT"] = dt("dbg_yT", [128, 6, S], BF16, kind="ExternalOutput").ap()
        dbg["x1"] = dt("dbg_x1", [S, D], F32, kind="ExternalOutput").ap()
        dbg["gT"] = dt("dbg_gT", [24, 128, S], BF16, kind="ExternalOutput").ap()

    w_in_r = w_in.rearrange("(kc p) n -> p kc n", p=128)
    w_gate_r = w_gate.rearrange("(kc p) n -> p kc n", p=128)
    w_out_r = w_out.rearrange("(kc p) n -> p kc n", p=128)
    w_a_r = w_a.rearrange("(kc p) n -> p kc n", p=128)
    w_b_r = w_b.rearrange("(kc p) n -> p kc n", p=128)
    w_up_r = w_up.rearrange("(kc p) n -> p kc n", p=128)
    w_down_r = w_down.rearrange("(kc p) n -> p kc n", p=128)

    with contextlib.ExitStack() as stack:
        arena_t = stack.enter_context(nc.sbuf_tensor("arena", [128, ARENA_F32], F32))
        small_t = stack.enter_context(nc.sbuf_tensor("small", [128, 768], F32))
        ps_t = stack.enter_context(nc.psum_tensor("ps", [128, 8, 512], F32))
        P = Prog(nc, stack)
        ar = Arena(arena_t[:])
        ps = ps_t[:]

        sm = small_t[:]
        ident = sm[:, 0:64].bitcast(BF16)
        masks = sm[:, 64:320].bitcast(BF16).rearrange("p (a b) -> p a b", a=4)
        onesb = sm[:, 320:384].bitcast(BF16)
        esink = sm[:, 384:388]
        neghalf = sm[:, 388:392]
        stat = sm[:, 392:456]
        bgt = sm[:, 456:472]
        permT = sm[:, 512:576].bitcast(BF16)
        cw = sm[:, 576:720].rearrange("p (a b) -> p a b", a=3)
        cb = sm[:, 720:768]

        P.dma("gpsimd", "c_ident", lambda e: e.dma_start(out=ident, in_=c_ident), writes=["ident"])
        P.dma("gpsimd", "c_masks", lambda e: e.dma_start(out=sm[:, 64:320].bitcast(BF16), in_=c_masks), writes=["masks"])
        P.dma("gpsimd", "c_perm", lambda e: e.dma_start(out=permT, in_=c_perm), writes=["permT"])
        P.op("vector", lambda e: e.memset(onesb, 1.0), writes=["ones"])
        P.op("vector", lambda e: e.memset(neghalf, -0.5), writes=["neghalf"])
        sink2 = sink.rearrange("(t two) -> two t", two=2)
        P.dma("sync", "sink", lambda e: e.dma_start(out=esink[0:64, :], in_=sink2[0, :].partition_broadcast(64), allow_slow_non_contiguous=True), writes=["esink"])
        P.dma("sync", "sink", lambda e: e.dma_start(out=esink[64:128, :], in_=sink2[1, :].partition_broadcast(64), allow_slow_non_contiguous=True), writes=["esink2"])
        P.op("scalar", lambda e: e.activation(out=esink, in_=esink, func=AF.Exp), reads=["esink", "esink2"], writes=["esinkx"])
        esT = sm[:, 472:480]
        P.dma("sync", "esT", lambda e: e.dma_start(out=esT, in_=sink.partition_broadcast(128)), writes=["esT0"])
        P.op("scalar", lambda e: e.activation(out=esT, in_=esT, func=AF.Exp), reads=["esT0"], writes=["esT"])

        def rstd_ops(ssq, rstd, n, tag):
            P.op("gpsimd", lambda e: e.tensor_scalar(out=ssq, in0=ssq, scalar1=1.0 / D, scalar2=EPS, op0=ALU.mult, op1=ALU.add),
                 reads=[tag + "ssq"], writes=[tag + "ssq"])
            P.op("gpsimd", lambda e: e.tensor_tensor(out=rstd, in0=ssq, in1=neghalf[:, 0:n], op=ALU.pow),
                 reads=[tag + "ssq", "neghalf"], writes=[tag + "rstd"])

        def norm_transpose(src_rows, gain_b, n_half, xin, hb, junk, tpbanks, dst_fn, tag, part="both"):
            ns = list(range(n_half)) if isinstance(n_half, int) else list(n_half)
            for n in ns:
                sl = n % len(xin)
                xi = xin[sl]
                if part == "tr":
                    break
                P.dma("sync", tag + "xin%d" % sl, lambda e, n=n, xi=xi: e.dma_start(out=xi, in_=src_rows(n)), writes=[tag + "xin%d" % sl])
                ssq = stat[:, 32 + 4 * sl:32 + 4 * sl + 2]
                rstd = stat[:, 48 + 4 * sl:48 + 4 * sl + 2]
                for s in range(2):
                    P.op("scalar", lambda e, s=s, xi=xi, ssq=ssq: e.activation(out=junk, in_=xi[:, s, :], func=AF.Square, accum_out=ssq[:, s:s + 1]),
                         reads=[tag + "xin%d" % sl], writes=[tag + "junk", tag + "ssq%d_%d" % (sl, s), tag + "ms%d" % sl])
                P.op("gpsimd", lambda e, ssq=ssq: e.tensor_scalar(out=ssq, in0=ssq, scalar1=1.0 / D, scalar2=EPS, op0=ALU.mult, op1=ALU.add),
                     reads=[tag + "ssq%d_0" % sl, tag + "ssq%d_1" % sl], writes=[tag + "ms%d" % sl])
                P.op("gpsimd", lambda e, ssq=ssq, rstd=rstd: e.tensor_tensor(out=rstd, in0=ssq, in1=neghalf[:, 0:2], op=ALU.pow),
                     reads=[tag + "ms%d" % sl, "neghalf"], writes=[tag + "rstd%d" % sl])
                for s in range(2):
                    hbs = hb[sl][:, s, :]
                    P.op("vector", lambda e, s=s, xi=xi, rstd=rstd, hbs=hbs: e.scalar_tensor_tensor(
                        out=hbs, in0=xi[:, s, :], scalar=rstd[:, s:s + 1], in1=gain_b, op0=ALU.mult, op1=ALU.mult),
                        reads=[tag + "xin%d" % sl, tag + "rstd%d" % sl, tag + "gain"], writes=[tag + "hb%d_%d" % (sl, s)])
                if part == "both":
                    norm_tr_part(n, sl, hb, tpbanks, dst_fn, tag)
            if part == "tr":
                for n in ns:
                    norm_tr_part(n, n % len(xin), hb, tpbanks, dst_fn, tag)

        def norm_tr_part(n, sl, hb, tpbanks, dst_fn, tag):
            if True:
                for s in range(2):
                    bk = (2 * n + s) % 2
                    tp = tpbanks[bk]
                    hbs = hb[sl][:, s, :]

                    def tr(e, tp=tp, hbs=hbs):
                        last = None
                        for kc in range(8):
                            last = e.transpose(out=tp[:, kc, :], in_=hbs[:, kc * 128:(kc + 1) * 128], identity=ident)
                        return last
                    P.op("tensor", tr, reads=[tag + "hb%d_%d" % (sl, s), "ident"], writes=[tag + "tp%d" % bk])
                    dst, dres = dst_fn(n, s)
                    eng = "scalar" if s == 0 else "vector"
                    if eng == "scalar":
                        P.op("scalar", lambda e, tp=tp, dst=dst: e.activation(out=dst, in_=tp, func=AF.Copy),
                             reads=[tag + "tp%d" % bk], writes=[dres])
                    else:
                        P.op("vector", lambda e, tp=tp, dst=dst: e.tensor_copy(out=dst, in_=tp),
                             reads=[tag + "tp%d" % bk], writes=[dres])

        tpb = [ps[:, 6, :].bitcast(BF16).rearrange("p (a b) -> p a b", a=8),
               ps[:, 7, :].bitcast(BF16).rearrange("p (a b) -> p a b", a=8)]

        ar.reset()
        wqk = ar.bf16([128, 8, 18, 128])
        qbf = [ar.bf16([128, 512]) for _ in range(2)]
        wv = ar.bf16([128, 8, 896])
        cosT = ar.f32([128, S])
        sinT = ar.f32([128, S])
        gpre = ar.f32([128, D])
        xin = [ar.f32([128, 2, D]) for _ in range(2)]
        junk = ar.bf16([128, D])
        hb = [ar.bf16([128, 2, D]) for _ in range(2)]
        hT = [ar.bf16([128, 8, 512]) for _ in range(2)]
        ra = [ar.f32([128, 512]) for _ in range(2)]
        rb = [ar.f32([128, 512]) for _ in range(2)]
        ro = [ar.bf16([128, 512]) for _ in range(3)]
        vo = [ar.bf16([128, 2, 448]) for _ in range(2)]

        P.dma("sync", "gpre", lambda e: e.dma_start(out=gpre, in_=n_pre.partition_broadcast(128)), writes=["p1gain"])
        col0 = [128 * t for t in range(4)] + [512, 576] + [768 + 128 * t for t in range(6)] + [1536 + 128 * t for t in range(6)]
        for t in range(18):
            c0 = col0[t]
            if t in (4, 5):
                P.dma("gpsimd", "wqk%d" % t, lambda e, t=t, c0=c0: e.dma_start(out=wqk[:, :, t, 0:64], in_=w_in_r[:, :, c0:c0 + 64]), writes=["wqk%da" % t])
                P.dma("gpsimd", "wqk%d" % t, lambda e, t=t, c0=c0: e.dma_start(out=wqk[:, :, t, 64:128], in_=w_in_r[:, :, c0:c0 + 64]), writes=["wqk%db" % t])
            else:
                P.dma("gpsimd", "wqk%d" % t, lambda e, t=t, c0=c0: e.dma_start(out=wqk[:, :, t, :], in_=w_in_r[:, :, c0:c0 + 128]), writes=["wqk%da" % t])
            if t == 0:
                P.dma("gpsimd", "cos", lambda e: e.dma_start(out=cosT, in_=c_cos), writes=["cos"])
                P.dma("gpsimd", "sin", lambda e: e.dma_start(out=sinT, in_=c_sin), writes=["sin"])
            if t == 3:
                P.dma("gpsimd", "wv", lambda e: e.dma_start(out=wv[:, :, 0:128], in_=w_in_r[:, :, 640:768]), writes=["wva"])
                P.dma("gpsimd", "wv", lambda e: e.dma_start(out=wv[:, :, 128:896], in_=w_in_r[:, :, 2304:3072]), writes=["wvb"])
        identF = ar.f32([128, 128])
        rowsA = ar.f32([128, 128])
        rowsB = ar.f32([80, 128]) if False else ar.f32([128, 128])
        P.dma("sync", "identF", lambda e: e.dma_start(out=identF, in_=c_ident), writes=["identF"])
        cw_rows = conv_w.rearrange("t (c p) -> (t c) p", p=128)
        P.dma("sync", "rowsA", lambda e: e.dma_start(out=rowsA, in_=cw_rows[0:128, :]), writes=["rowsA"])
        P.op("vector", lambda e: e.memset(rowsB, 0.0), writes=["rowsB0"])
        P.dma("sync", "rowsB", lambda e: e.dma_start(out=rowsB[0:16, :], in_=cw_rows[128:144, :]), reads=["rowsB0"], writes=["rowsB1"])
        P.dma("sync", "rowsB", lambda e: e.dma_start(out=rowsB[16:64, :], in_=conv_b.rearrange("(c p) -> c p", p=128)), reads=["rowsB0"], writes=["rowsB2"])
        P.dma("sync", "rowsB", lambda e: e.dma_start(out=rowsB[64:80, :], in_=b_gate.rearrange("(c p) -> c p", p=128)), reads=["rowsB0"], writes=["rowsB3"])
        P.op("tensor", lambda e: e.transpose(out=ps[:, 4, 0:128], in_=rowsA, identity=identF), reads=["rowsA", "identF"], writes=["vps"])
        P.op("vector", lambda e: e.tensor_copy(out=sm[:, 576:704], in_=ps[:, 4, 0:128]), reads=["vps"], writes=["cwA"])
        P.op("tensor", lambda e: e.transpose(out=ps[:, 5, 0:128], in_=rowsB, identity=identF), reads=["rowsB0", "rowsB1", "rowsB2", "rowsB3", "identF"], writes=["vps"])
        P.op("vector", lambda e: e.tensor_copy(out=sm[:, 704:768], in_=ps[:, 5, 0:64]), reads=["vps"], writes=["cwB"])
        P.op("vector", lambda e: e.tensor_copy(out=bgt, in_=ps[:, 5, 64:80]), reads=["vps"], writes=["bgt"])

        x_r = x.rearrange("(n s p) d -> n p s d", p=128, s=2)
        qps = [[ps[:, 0, :], ps[:, 1, :]], [ps[:, 2, :], ps[:, 3, :]]]
        vps = ps[:, 4:6, 0:448]

        for tt in range(8):
            t0 = tt * 512
            hts = hT[tt % 2]

            def dst_fn(n, s, hts=hts, tt=tt):
                sub = (n % 2) * 2 + s
                return hts[:, :, sub * 128:(sub + 1) * 128], "p1hT%d_%d" % (tt % 2, sub)
            if tt == 0:
                norm_transpose(lambda n: x_r[n], gpre, 2, xin, hb, junk, tpb, None, "p1", part="norm")
            norm_transpose(None, gpre, 2, xin, hb, junk, tpb, (lambda n, s, tt=tt, f=dst_fn: f(n, s)), "p1", part="tr")
            if tt + 1 < 8:
                norm_transpose(lambda n, tt=tt: x_r[(tt + 1) * 2 + n], gpre, 2, xin, hb, junk, tpb, None, "p1", part="norm")
            hres = ["p1hT%d_%d" % (tt % 2, sub) for sub in range(4)]
            P.dma("sync", "hTst%d" % (tt % 2), lambda e, hts=hts, t0=t0: e.dma_start(out=hT_d[:, :, t0:t0 + 512], in_=hts), reads=hres)
            def straight(t, hts=hts, tt=tt):
                st = (tt * 18 + t) % 2
                bank = qps[st][0]

                def mm(e, t=t, bank=bank, hts=hts):
                    last = None
                    for kc in range(8):
                        last = e.matmul(bank, lhsT=wqk[:, kc, t, :], rhs=hts[:, kc, :], start=(kc == 0), stop=(kc == 7))
                    return last
                wr = ["wqk%da" % t] + (["wqk%db" % t] if t in (4, 5) else [])
                P.op("tensor", mm, reads=hres + wr, writes=["qps%d_0" % st])
                P.op("scalar", lambda e, st=st: e.activation(out=qbf[st], in_=qps[st][0], func=AF.Copy), reads=["qps%d_0" % st], writes=["qbf%d" % st])

            def rotate(t, tt=tt, t0=t0):
                st = (tt * 18 + t) % 2
                P.op("tensor", lambda e, st=st: e.matmul(qps[st][1], lhsT=permT, rhs=qbf[st], start=True, stop=True),
                     reads=["permT", "qbf%d" % st], writes=["qps%d_1" % st])
                P.op("vector", lambda e, st=st, t0=t0: e.tensor_tensor(out=ra[st], in0=qps[st][0], in1=cosT[:, t0:t0 + 512], op=ALU.mult),
                     reads=["qps%d_0" % st, "cos", "qbf%d" % st], writes=["ra%d" % st])
                P.op("vector", lambda e, st=st, t0=t0: e.tensor_tensor(out=rb[st], in0=qps[st][1], in1=sinT[:, t0:t0 + 512], op=ALU.mult),
                     reads=["qps%d_1" % st, "sin"], writes=["rb%d" % st])
                so = (tt * 18 + t) % 3
                P.op("gpsimd", lambda e, st=st, so=so: e.tensor_tensor(out=ro[so], in0=ra[st], in1=rb[st], op=ALU.add),
                     reads=["ra%d" % st, "rb%d" % st], writes=["ro%d" % so])
                P.dma("sync", "rost%d" % so, lambda e, so=so, t=t, t0=t0: e.dma_start(out=qk_d[t, :, t0:t0 + 512], in_=ro[so]), reads=["ro%d" % so])

            for t in range(19):
                if t < 18:
                    straight(t)
                if t >= 1:
                    rotate(t - 1)
            for sub in range(4):
                sv = sub % 2

                vb = 4 + 2 * sv
                vres = ["vps"] if sv == 0 else ["p1tp0", "p1tp1"]

                def mmv(e, sub=sub, hts=hts, vb=vb):
                    last = None
                    for n in range(2):
                        for kc in range(8):
                            last = e.matmul(ps[:, vb + n, 0:448], lhsT=hts[:, kc, sub * 128:(sub + 1) * 128], rhs=wv[:, kc, n * 448:(n + 1) * 448],
                                            start=(kc == 0), stop=(kc == 7))
                    return last
                P.op("tensor", mmv, reads=hres + ["wva", "wvb"], writes=vres)
                P.op("scalar", lambda e, sv=sv, vb=vb: e.activation(out=vo[sv], in_=ps[:, vb:vb + 2, 0:448], func=AF.Copy), reads=vres, writes=["vo%d" % sv])
                P.dma("sync", "vost%d" % sv, lambda e, sv=sv, sub=sub, t0=t0: e.dma_start(
                    out=v_d[t0 + sub * 128:t0 + (sub + 1) * 128, :].rearrange("p (a b) -> p a b", a=2), in_=vo[sv]), reads=["vo%d" % sv])
        P.barrier()
        if debug:
            P.dma("sync", "dbg", lambda e: e.dma_start(out=dbg["qk"], in_=qk_d))
            P.dma("sync", "dbg", lambda e: e.dma_start(out=dbg["v"], in_=v_d))
            P.barrier()

        ar.reset()
        yT = ar.bf16([128, 6, S])
        yT_off = ar.off
        if upto >= 2:
            Q = [ar.bf16([128, S]) for _ in range(2)]
            KZ = [[ar.bf16([128, S]) for _ in range(2)] for _ in range(2)]
            VXs = [ar.bf16([128, 48, 128]) for _ in range(2)]
            VYs = [ar.bf16([128, 48, 128]) for _ in range(2)]
            acc = ar.f32([128, 2, S])
            accX = acc[:, 0, :]
            accY = acc[:, 1, :]
            NR = 5
            Pr = [ar.bf16([128, 2, 3, 128]) for _ in range(NR)]
            rdA = [ar.f32([128, 2]) for _ in range(2)]
            ytok = [ar.bf16([128, 2, 64]) for _ in range(2)]
            dsbw = ar.f32([128, 1024])

            WG_OFF = ARENA_F32 - 8192
            wg = ar.bf16_at(WG_OFF, [128, 8, 2048])
            WG_K0 = -(-(ar.off - WG_OFF) // 1024) if ar.off > WG_OFF else 0
            for kc in range(WG_K0, 8):
                P.dma("gpsimd", "wg", lambda e, kc=kc: e.dma_start(out=wg[:, kc, :], in_=w_gate_r[:, kc, :]), writes=["wg%d" % kc])
            for sl in range(2):
                P.op("gpsimd", lambda e, sl=sl: e.memset(KZ[sl][0][64:128, :], 0.0), writes=["kz%d_0z" % sl])
                P.op("gpsimd", lambda e, sl=sl: e.memset(KZ[sl][1][0:64, :], 0.0), writes=["kz%d_1z" % sl])
            for vs in range(2):
                P.op("vector", lambda e, vs=vs: e.memset(VXs[vs][:, :, 64:128], 1.0), writes=["vx1_%d" % vs])
                P.op("vector", lambda e, vs=vs: e.memset(VYs[vs][:, :, 0:64], 1.0), writes=["vy1_%d" % vs])

            sps = [ps[:, 0:2, :], ps[:, 2:4, :]]
            ops = [ps[:, 4, 0:256].rearrange("p (a b) -> p a b", a=2), ps[:, 5, 0:256].rearrange("p (a b) -> p a b", a=2)]
            opsA = [ps[:, 4, 0:130].rearrange("p (a b) -> p a b", a=2), ps[:, 5, 0:130].rearrange("p (a b) -> p a b", a=2)]

            units = [("A", t, 0, 0) for t in range(4)] + [("B", 0, p, g) for p in range(2) for g in range(3)]
            DIL = [1, 4, 16]
            state = {"pstep": 0, "ostep": 0, "tq": []}

            def load_unit(ui):
                kind, t, p, g = units[ui]
                sl = ui % 2
                if kind == "A":
                    qi, ki, vc = t, 4 + t // 2, (t // 2) * 64
                else:
                    qi, ki, vc = 6 + 2 * g + p, 12 + 2 * g + p, 128 + g * 256 + p * 128
                P.dma("sync", "Q%d" % sl, lambda e: e.dma_start(out=Q[sl], in_=qk_d[qi]), writes=["Q%d" % sl])
                P.dma("sync", "K%d" % sl, lambda e: e.dma_start(out=KZ[sl][0][0:64, :], in_=qk_d[ki, 0:64, :]), writes=["kz%d_0" % sl])
                P.dma("sync", "K%d" % sl, lambda e: e.dma_start(out=KZ[sl][1][64:128, :], in_=qk_d[ki, 64:128, :]), writes=["kz%d_1" % sl])

            def load_v(ui):
                kind, t, p, g = units[ui]
                vs = ui % 2
                VX, VY = VXs[vs], VYs[vs]
                kx, ky = "VX%d" % vs, "VY%d" % vs
                if kind == "A":
                    vc = (t // 2) * 64
                    src = v_d[:, vc:vc + 64].rearrange("(m k) c -> k m c", k=128)
                    for q4 in range(4):
                        P.dma("sync", kx, lambda e, q4=q4: e.dma_start(out=VX[:, 8 * q4:8 * q4 + 8, 0:64], in_=src[:, 8 * q4:8 * q4 + 8, :]), writes=[kx] if q4 == 0 else [])
                        P.dma("sync", ky, lambda e, q4=q4: e.dma_start(out=VY[:, 8 * q4:8 * q4 + 8, 64:128], in_=src[:, 8 * q4:8 * q4 + 8, :]), writes=[ky] if q4 == 0 else [])
                    for key in (kx, ky):
                        ent = P.dsem[key]
                        P.res[key] = {"w": (ent[0], ent[1]), "r": {}}
                    return
                d = DIL[g]
                L = S // d
                M = L // 128
                vc = 128 + g * 256 + p * 128
                first = True
                for r in range(d):
                    cb = r * (M + 1)
                    for (vt, co, key) in ((VX, 0, kx), (VY, 64, ky)):
                        cs = vc + co
                        wr = [key] if first else []
                        o0, i0 = vt[:, cb, co:co + 64], v_d[r:r + d * 127 + 1:d, cs:cs + 64]
                        P.dma("sync", key, lambda e, o0=o0, i0=i0: e.dma_start(out=o0, in_=i0), writes=wr)
                        if M > 1:
                            a = r + d * 64
                            i1f = v_d[a:a + d * (128 * (M - 1) - 1) + 1:d, cs:cs + 64].rearrange("(m k) c -> k m c", k=128)
                            for m0 in range(0, M - 1, 8):
                                m1 = min(m0 + 8, M - 1)
                                o1 = vt[:, cb + 1 + m0:cb + 1 + m1, co:co + 64]
                                i1 = i1f[:, m0:m1, :]
                                P.dma("sync", key, lambda e, o1=o1, i1=i1: e.dma_start(out=o1, in_=i1))
                        a = r + d * (L - 128)
                        o2, i2 = vt[:, cb + M, co:co + 64], v_d[a:a + d * 127 + 1:d, cs:cs + 64]
                        P.dma("sync", key, lambda e, o2=o2, i2=i2: e.dma_start(out=o2, in_=i2))
                    first = False
                for key in (kx, ky):
                    ent = P.dsem[key]
                    P.res[key] = {"w": (ent[0], ent[1]), "r": {}}

            def do_unit(ui):
                kind, t, p, g = units[ui]
                sl = ui % 2
                VX, VY = VXs[sl], VYs[sl]
                if ui + 1 < len(units):
                    load_unit(ui + 1)
                    load_v(ui + 1)
                Qs, Ka, Kb = Q[sl], KZ[sl][0], KZ[sl][1]
                kres = ["Q%d" % sl, "kz%d_0" % sl, "kz%d_1" % sl, "kz%d_0z" % sl, "kz%d_1z" % sl]
                if kind == "A":
                    d, M, nseq = 1, 32, 1
                else:
                    d = DIL[g]
                    M = (S // d) // 128
                    nseq = d
                steps = []
                if kind == "A":
                    steps = [(0, m) for m in range(32)]
                else:
                    steps = [(r, m) for r in range(nseq) for m in range(M + 1)]
                pring = {}
                pending = []
                LAG = 2

                def tok(r, pos0, n):
                    a = r + d * pos0
                    return slice(a, a + d * (n - 1) + 1, d) if d > 1 else slice(a, a + n)

                def emit_S(r, m):
                    pi = state["pstep"] % NR
                    sb = state["pstep"] % 2
                    state["pstep"] += 1
                    pring[(r, m)] = pi
                    if kind == "A":
                        j0, j1 = max(m - 1, 0), min(m + 1, 31)
                        slot0 = j0 - m + 1
                        kpos = m * 128
                    else:
                        j0, j1 = max(m - 1, 0), min(m, M - 1)
                        slot0 = j0 - m + 1
                        L = S // d
                        kpos = 0 if m == 0 else (L - 128 if m == M else 128 * m - 64)
                    nq = j1 - j0 + 1
                    ksl = tok(r, kpos, 128)
                    qsl = tok(r, j0 * 128, nq * 128)
                    c0, c1 = slot0 * 128, (slot0 + nq) * 128
                    for hh in range(2):
                        Kt = Ka if hh == 0 else Kb
                        P.op("tensor", lambda e, hh=hh, Kt=Kt, sb=sb: e.matmul(sps[sb][:, hh, c0:c1], lhsT=Kt[:, ksl], rhs=Qs[:, qsl], start=True, stop=True),
                             reads=kres, writes=["sps%d_%d" % (sb, hh)])
                    Pt = Pr[pi]
                    Pf = Pt.rearrange("p h s q -> p h (s q)")
                    P.op("scalar", lambda e, sb=sb, Pf=Pf: e.activation(out=Pf[:, :, c0:c1], in_=sps[sb][:, :, c0:c1], func=AF.Exp, scale=0.125),
                         reads=["sps%d_0" % sb, "sps%d_1" % sb], writes=["P%d_0" % pi, "P%d_1" % pi])
                    if kind == "A":
                        mlist = []
                        if m > 0:
                            mlist.append((0, 0))
                        if m < 31:
                            mlist.append((2, 1))
                        both = len(mlist) == 2
                        ssl, msl = (slice(0, 3, 2), slice(0, 2)) if both else (slice(mlist[0][0], mlist[0][0] + 1), slice(mlist[0][1], mlist[0][1] + 1))
                    else:
                        if 0 < m < M:
                            ssl, msl = slice(0, 2), slice(0, 2)
                        elif m == 0:
                            ssl, msl = slice(1, 2), slice(2, 3)
                        else:
                            ssl, msl = slice(0, 1), slice(3, 4)
                    def do_masks():
                        for hh, eng in ((0, "gpsimd"), (1, "vector")):
                            P.op(eng, lambda e, Pt=Pt, hh=hh, ssl=ssl, msl=msl: e.tensor_tensor(out=Pt[:, hh, ssl, :], in0=Pt[:, hh, ssl, :], in1=masks[:, msl, :], op=ALU.mult),
                                 reads=["P%d_%d" % (pi, hh), "masks"], writes=["P%d_%d" % (pi, hh)])
                    return do_masks

                def flush_tr():
                    while state["tq"]:
                        jj, obb = state["tq"].pop(0)
                        tb = jj % 2
                        tpa = tpb[tb][:, 0, :]
                        yk2 = ytok[obb]
                        P.op("tensor", lambda e, yk2=yk2, tpa=tpa: e.transpose(out=tpa, in_=yk2.rearrange("p h d -> p (h d)"), identity=ident),
                             reads=["ytok%d" % obb, "ident"], writes=["tpA%d" % tb])
                        P.op("scalar", lambda e, tpa=tpa, jj=jj: e.activation(out=yT[:, t, jj * 128:(jj + 1) * 128], in_=tpa, func=AF.Copy),
                             reads=["tpA%d" % tb], writes=["yTa%d_%d" % (t, jj)])

                def emit_PV(r, j):
                    ob = state["ostep"] % 2
                    state["ostep"] += 1
                    if kind == "A":
                        ms = [m for m in (j - 1, j, j + 1) if 0 <= m <= 31]
                        oa = opsA[ob]
                        for hh in range(2):
                            def pva(e, hh=hh, oa=oa):
                                last = None
                                for i, m in enumerate(ms):
                                    last = e.matmul(oa[:, hh, :], lhsT=Pr[pring[(r, m)]][:, hh, j - m + 1, :], rhs=VX[:, m, 0:65],
                                                    start=(i == 0), stop=(i == len(ms) - 1))
                                return last
                            P.op("tensor", pva, reads=["P%d_%d" % (pring[(r, m)], hh) for m in ms] + ["VX%d" % sl, "vx1_%d" % sl], writes=["ops%d_%d" % (ob, hh)])
                        flush_tr()
                        rd = rdA[ob]
                        yk = ytok[ob]
                        ores = ["ops%d_0" % ob, "ops%d_1" % ob]
                        P.op("vector", lambda e, oa=oa, rd=rd: e.tensor_tensor(out=rd, in0=oa[:, :, 64], in1=esT[:, 2 * t:2 * t + 2], op=ALU.add),
                             reads=ores + ["esT"], writes=["rdA%d" % ob])
                        P.op("vector", lambda e, rd=rd: e.reciprocal(out=rd, in_=rd), reads=["rdA%d" % ob], writes=["rdA%d" % ob])
                        P.op("vector", lambda e, oa=oa, rd=rd, yk=yk: e.tensor_tensor(out=yk, in0=oa[:, :, 0:64], in1=rd.unsqueeze(2).broadcast_to([128, 2, 64]), op=ALU.mult),
                             reads=ores + ["rdA%d" % ob], writes=["ytok%d" % ob])
                        state["tq"].append((j, ob))
                        return
                    ms = [j, j + 1]
                    chunk = lambda m: r * (M + 1) + m
                    for hh in range(2):
                        Vt = VX if hh == 0 else VY

                        def pv(e, hh=hh, Vt=Vt, ob=ob):
                            last = None
                            for i, m in enumerate(ms):
                                last = e.matmul(ops[ob][:, hh, :], lhsT=Vt[:, chunk(m), :], rhs=Pr[pring[(r, m)]][:, hh, j - m + 1, :],
                                                start=(i == 0), stop=(i == len(ms) - 1))
                            return last
                        P.op("tensor", pv, reads=["P%d_%d" % (pring[(r, m)], hh) for m in ms] + ["VX%d" % sl, "VY%d" % sl, "vx1_%d" % sl, "vy1_%d" % sl], writes=["ops%d_%d" % (ob, hh)])
                    o = ops[ob]
                    if True:
                        cols = tok(r, j * 128, 128)
                        if g == 0:
                            P.op("vector", lambda e, o=o: e.tensor_copy(out=acc[:, :, cols], in_=o),
                                 reads=["ops%d_0" % ob, "ops%d_1" % ob], writes=["accX", "accY"])
                        else:
                            P.op("vector", lambda e, o=o: e.tensor_tensor(out=acc[:, :, cols], in0=o, in1=acc[:, :, cols], op=ALU.add),
                                 reads=["ops%d_0" % ob, "ops%d_1" % ob, "accX", "accY"], writes=["accX", "accY"])

                for (r, m) in steps:
                    masks_later = emit_S(r, m)
                    pending = [(rr, jj, c - 1) for (rr, jj, c) in pending]
                    if kind == "A":
                        if m >= 1:
                            pending.append((r, m - 1, LAG))
                        if m == 31:
                            pending.append((r, 31, LAG))
                    else:
                        if m >= 1:
                            pending.append((r, m - 1, LAG))
                    while pending and pending[0][2] <= 0:
                        rr, jj, _ = pending.pop(0)
                        emit_PV(rr, jj)
                    masks_later()
                for (rr, jj, _) in pending:
                    emit_PV(rr, jj)
                pending = []
                flush_tr()

                if kind == "B" and g == 2:
                    for c in range(4):
                        cs = slice(c * 1024, (c + 1) * 1024)
                        P.op("scalar", lambda e, cs=cs: e.activation(out=dsbw[0:64, :], in_=accX[64:128, cs], func=AF.Copy),
                             reads=["accX"], writes=["dsbw_lo"])
                        P.op("scalar", lambda e, cs=cs: e.activation(out=dsbw[64:128, :], in_=accY[0:64, cs], func=AF.Copy),
                             reads=["accY"], writes=["dsbw_hi"])
                        P.op("scalar", lambda e: e.activation(out=dsbw, in_=dsbw, func=AF.Ln), reads=["dsbw_lo", "dsbw_hi"], writes=["dsbw_lo", "dsbw_hi"])
                        P.op("scalar", lambda e: e.activation(out=dsbw, in_=dsbw, func=AF.Exp, scale=-1.0), reads=["dsbw_lo", "dsbw_hi"], writes=["dsbw_lo", "dsbw_hi"])
                        P.op("vector", lambda e, cs=cs: e.tensor_tensor(out=yT[0:64, 4 + p, cs], in0=accX[0:64, cs], in1=dsbw[0:64, :], op=ALU.mult),
                             reads=["accX", "dsbw_lo"], writes=["yTBa%d_%d" % (p, c)])
                        P.op("vector", lambda e, cs=cs: e.tensor_tensor(out=yT[64:128, 4 + p, cs], in0=accY[64:128, cs], in1=dsbw[64:128, :], op=ALU.mult),
                             reads=["accY", "dsbw_hi"], writes=["yTBb%d_%d" % (p, c)])

            load_unit(0)
            load_v(0)
            for ui in range(len(units)):
                do_unit(ui)
            P.barrier()
            if debug:
                P.dma("sync", "dbg", lambda e: e.dma_start(out=dbg["yT"], in_=yT))
                P.barrier()

        if upto >= 3:
            ar.reset(yT_off)
            wg = ar.bf16_at(WG_OFF, [128, 8, 2048])
            wa = ar.bf16([128, 4, D])
            wb = ar.bf16([128, 2, D])
            wo = ar.bf16([128, 8, D])
            hTt = [ar.bf16([128, 8, 512]) for _ in range(2)]
            gpost = ar.f32([128, D])
            sA = [ar.f32([128, 512]) for _ in range(2)]
            sB = [ar.f32([128, 512]) for _ in range(2)]
            m1 = [ar.f32([128, 512]) for _ in range(2)]
            m2 = [ar.f32([128, 512]) for _ in range(2)]
            mT = [ar.bf16([128, 8, 512]) for _ in range(2)]
            xres = [ar.f32([128, D]) for _ in range(2)]
            ytmp = [ar.f32([128, D]) for _ in range(2)]
            junk3 = ar.bf16([128, D])
            assert WG_K0 == 8 and ar.off <= WG_OFF
            for dc in range(8):
                dsl_ = slice(dc * 128, (dc + 1) * 128)
                dsl2 = slice(1024 + dc * 128, 1024 + (dc + 1) * 128)
                P.dma("gpsimd", "wgdc%d" % dc, lambda e, dsl_=dsl_: e.dma_start(out=wg[:, :, dsl_], in_=w_gate_r[:, :, dsl_]), writes=["wgA%d" % dc])
                P.dma("gpsimd", "wgdc%d" % dc, lambda e, dsl2=dsl2: e.dma_start(out=wg[:, :, dsl2], in_=w_gate_r[:, :, dsl2]), writes=["wgB%d" % dc])
                P.dma("gpsimd", "wgdc%d" % dc, lambda e, dsl_=dsl_: e.dma_start(out=wa[:, :, dsl_], in_=w_a_r[:, :, dsl_]), writes=["wa%d" % dc])
                P.dma("gpsimd", "wgdc%d" % dc, lambda e, dsl_=dsl_: e.dma_start(out=wb[:, :, dsl_], in_=w_b_r[:, :, dsl_]), writes=["wb%d" % dc])
            for i in range(2):
                P.dma("gpsimd", "wo", lambda e, i=i: e.dma_start(out=wo[:, :, i * 512:(i + 1) * 512], in_=w_out_r[:, :, i * 512:(i + 1) * 512]), writes=["wo%d" % i])
            P.dma("sync", "gpost", lambda e: e.dma_start(out=gpost, in_=n_post.partition_broadcast(128)), writes=["gpost"])
            mb = [ps[:, i, :] for i in range(4)]
            outb = [ps[:, 4:6, :], ps[:, 6:8, :]]
            osub = 0
            for tt in range(8):
                t0 = tt * 512
                sl = tt % 2
                if tt == 0:
                    P.dma("sync", "hTt0", lambda e: e.dma_start(out=hTt[0], in_=hT_d[:, :, 0:512]), writes=["hTt0"])
                if tt + 1 < 8:
                    P.dma("sync", "hTt%d" % (1 - sl), lambda e, sl=sl, t0=t0: e.dma_start(out=hTt[1 - sl], in_=hT_d[:, :, t0 + 512:t0 + 1024]), writes=["hTt%d" % (1 - sl)])
                for dc in range(8):
                    s2 = dc % 2
                    dsl = slice(dc * 128, (dc + 1) * 128)

                    def mmg(e, off, bank, sl=sl, dc=dc):
                        last = None
                        for kc in range(8):
                            last = e.matmul(bank, lhsT=wg[:, kc, off + dc * 128:off + (dc + 1) * 128], rhs=hTt[sl][:, kc, :], start=(kc == 0), stop=(kc == 7))
                        return last
                    P.op("tensor", lambda e, f=mmg: f(e, 0, mb[0]), reads=["wgA%d" % dc, "hTt%d" % sl], writes=["mb0"])
                    P.op("tensor", lambda e, f=mmg: f(e, 1024, mb[1]), reads=["wgB%d" % dc, "hTt%d" % sl], writes=["mb1"])

                    def mma(e, dsl=dsl, t0=t0):
                        last = None
                        for kc in range(4):
                            last = e.matmul(mb[2], lhsT=wa[:, kc, dsl], rhs=yT[:, kc, t0:t0 + 512], start=(kc == 0), stop=(kc == 3))
                        return last
                    P.op("tensor", mma, reads=["wa%d" % dc], writes=["mb2"])

                    def mmb(e, dsl=dsl, t0=t0):
                        last = None
                        for kc in range(2):
                            last = e.matmul(mb[3], lhsT=wb[:, kc, dsl], rhs=yT[:, 4 + kc, t0:t0 + 512], start=(kc == 0), stop=(kc == 1))
                        return last
                    P.op("tensor", mmb, reads=["wb%d" % dc], writes=["mb3"])
                    P.op("scalar", lambda e, s2=s2, dc=dc: e.activation(out=sA[s2], in_=mb[0], func=AF.Sigmoid, bias=bgt[:, dc:dc + 1], scale=1.0),
                         reads=["mb0", "bgt"], writes=["sA%d" % s2])
                    P.op("scalar", lambda e, s2=s2, dc=dc: e.activation(out=sB[s2], in_=mb[1], func=AF.Sigmoid, bias=bgt[:, 8 + dc:9 + dc], scale=1.0),
                         reads=["mb1", "bgt"], writes=["sB%d" % s2])
                    P.op("vector", lambda e, s2=s2: e.tensor_tensor(out=m1[s2], in0=mb[2], in1=sA[s2], op=ALU.mult), reads=["mb2", "sA%d" % s2], writes=["m1_%d" % s2])
                    P.op("vector", lambda e, s2=s2: e.tensor_tensor(out=m2[s2], in0=mb[3], in1=sB[s2], op=ALU.mult), reads=["mb3", "sB%d" % s2], writes=["m2_%d" % s2])
                    P.op("gpsimd", lambda e, s2=s2, sl=sl, dc=dc: e.tensor_tensor(out=mT[sl][:, dc, :], in0=m1[s2], in1=m2[s2], op=ALU.add),
                         reads=["m1_%d" % s2, "m2_%d" % s2], writes=["mT%d_%d" % (sl, dc)])
                mres = ["mT%d_%d" % (sl, dc) for dc in range(8)]
                for sub in range(4):
                    ob = osub % 2
                    osub += 1
                    r0 = t0 + sub * 128
                    P.dma("sync", "xres%d" % ob, lambda e, ob=ob, r0=r0: e.dma_start(out=xres[ob], in_=x[r0:r0 + 128, :]), writes=["xres%d" % ob])

                    def mmo(e, ob=ob, sl=sl, sub=sub):
                        last = None
                        for n in range(2):
                            for c in range(8):
                                last = e.matmul(outb[ob][:, n, :], lhsT=mT[sl][:, c, sub * 128:(sub + 1) * 128], rhs=wo[:, c, n * 512:(n + 1) * 512],
                                                start=(c == 0), stop=(c == 7))
                        return last
                    P.op("tensor", mmo, reads=mres + ["wo0", "wo1"], writes=["outb%d" % ob])
                    ssq = stat[:, 16 + ob:17 + ob]
                    rs = stat[:, 20 + ob:21 + ob]
                    P.op("scalar", lambda e, ob=ob, ssq=ssq: e.activation(out=junk3.rearrange("p (a b) -> p a b", a=2), in_=outb[ob], func=AF.Square, accum_out=ssq),
                         reads=["outb%d" % ob], writes=["junk3", "p3ssq%d" % ob])
                    P.op("gpsimd", lambda e, ssq=ssq: e.tensor_scalar(out=ssq, in0=ssq, scalar1=1.0 / D, scalar2=EPS, op0=ALU.mult, op1=ALU.add),
                         reads=["p3ssq%d" % ob], writes=["p3ms%d" % ob])
                    P.op("gpsimd", lambda e, ssq=ssq, rs=rs: e.tensor_tensor(out=rs, in0=ssq, in1=neghalf[:, 0:1], op=ALU.pow),
                         reads=["p3ms%d" % ob, "neghalf"], writes=["p3rs%d" % ob])
                    P.op("vector", lambda e, ob=ob, rs=rs: e.scalar_tensor_tensor(out=ytmp[ob].rearrange("p (a b) -> p a b", a=2), in0=outb[ob], scalar=rs,
                                                                               in1=gpost.rearrange("p (a b) -> p a b", a=2), op0=ALU.mult, op1=ALU.mult),
                         reads=["outb%d" % ob, "p3rs%d" % ob, "gpost"], writes=["ytmp%d" % ob])
                    P.op("gpsimd", lambda e, ob=ob: e.tensor_tensor(out=ytmp[ob], in0=ytmp[ob], in1=xres[ob], op=ALU.add),
                         reads=["ytmp%d" % ob, "xres%d" % ob], writes=["ytmp%d" % ob])
                    P.dma("sync", "x1st%d" % ob, lambda e, ob=ob, r0=r0: e.dma_start(out=x1_d[r0:r0 + 128, :], in_=ytmp[ob]), reads=["ytmp%d" % ob])
            P.barrier()
            if debug:
                P.dma("sync", "dbg", lambda e: e.dma_start(out=dbg["x1"], in_=x1_d))
                P.barrier()

        if upto >= 4:
            ar.reset()
            h2T = ar.bf16([128, 8, S])
            h2_off = ar.off
            gfpre = ar.f32([128, D])
            xin4 = [ar.f32([128, 2, D]) for _ in range(4)]
            junk4 = ar.bf16([128, D])
            hb4 = [ar.bf16([128, 2, D]) for _ in range(4)]
            P.dma("sync", "gfpre", lambda e: e.dma_start(out=gfpre, in_=n_fpre.partition_broadcast(128)), writes=["p4gain"])
            x1_r = x1_d.rearrange("(n s p) d -> n p s d", p=128, s=2)

            def dst4(n, s):
                tk = n * 256 + s * 128
                return h2T[:, :, tk:tk + 128], "h2T_%d" % (n * 2 + s)
            norm_transpose(lambda n: x1_r[n], gfpre, [0, 1], xin4, hb4, junk4, tpb, None, "p4", part="norm")
            for n4 in range(16):
                norm_transpose(None, gfpre, [n4], xin4, hb4, junk4, tpb, dst4, "p4", part="tr")
                if n4 + 2 < 16:
                    norm_transpose(lambda n: x1_r[n], gfpre, [n4 + 2], xin4, hb4, junk4, tpb, None, "p4", part="norm")
            P.barrier()

            ar.reset(h2_off)
            wup = [ar.bf16([128, 8, 256]) for _ in range(3)]
            Cg = [ar.f32([128, S]) for _ in range(2)]
            Cu = [ar.f32([128, S]) for _ in range(2)]
            gout = [ar.bf16([128, S]) for _ in range(2)]
            edge = ar.f32([128, 2, 8])
            WD_OFF = ARENA_F32 - 12288
            wd = ar.bf16_at(WD_OFF, [128, 24, D])
            WD_C0 = -(-(ar.off - WD_OFF) // 512) if ar.off > WD_OFF else 0
            WD_C0 = -(-WD_C0 // 4) * 4
            wd_todo = list(range(WD_C0 // 4, 6))

            def load_wup(fc):
                sl = fc % 3
                P.dma("gpsimd", "wup%d" % sl, lambda e: e.dma_start(out=wup[sl][:, :, 0:128], in_=w_up_r[:, :, fc * 128:(fc + 1) * 128]), writes=["wup%d" % sl])
                P.dma("gpsimd", "wup%d" % sl, lambda e: e.dma_start(out=wup[sl][:, :, 128:256], in_=w_up_r[:, :, DFF + fc * 128:DFF + (fc + 1) * 128]), writes=["wup%db" % sl])
            load_wup(0)
            load_wup(1)
            ustep = 0
            for fc in range(24):
                if fc + 2 < 24:
                    load_wup(fc + 2)
                if fc >= 2 and wd_todo:
                    i_ = wd_todo.pop(0)
                    P.dma("gpsimd", "wd", lambda e, i=i_: e.dma_start(out=wd[:, 4 * i:4 * i + 4, :], in_=w_down_r[:, 4 * i:4 * i + 4, :]), writes=["wd%d" % i_])
                sl = fc % 3
                cs = fc % 2
                for tt in range(8):
                    t0 = tt * 512
                    ub = ustep % 4
                    ustep += 1
                    banks = (ps[:, 2 * ub, :], ps[:, 2 * ub + 1, :])
                    for gi in range(2):
                        def mmu(e, gi=gi, bank=banks[gi], sl=sl, t0=t0):
                            last = None
                            for kc in range(8):
                                last = e.matmul(bank, lhsT=wup[sl][:, kc, gi * 128:(gi + 1) * 128], rhs=h2T[:, kc, t0:t0 + 512], start=(kc == 0), stop=(kc == 7))
                            return last
                        P.op("tensor", mmu, reads=["wup%d" % sl, "wup%db" % sl], writes=["ub%d_%d" % (ub, gi)])
                    stages = {"act": [], "left": [], "right": [], "bnd": []}
                    for gi, Cb, nm0 in ((0, Cg[cs], "Cg%d" % cs), (1, Cu[cs], "Cu%d" % cs)):
                        nm = nm0 + "_h%d" % (tt // 4)
                        nmx = [nm0 + "_h0", nm0 + "_h1"] if tt == 4 else [nm]
                        ch = gi * 24 + fc
                        U = banks[gi]
                        ur = "ub%d_%d" % (ub, gi)
                        w0 = cw[:, 0, ch:ch + 1]
                        w1 = cw[:, 1, ch:ch + 1]
                        w2 = cw[:, 2, ch:ch + 1]
                        lo = 1 if tt == 0 else 0

                        def f_act(U=U, Cb=Cb, w1=w1, ch=ch, t0=t0, ur=ur, nm=nm, gi=gi, tt=tt):
                            P.op("scalar", lambda e: e.activation(out=Cb[:, t0:t0 + 512], in_=U, func=AF.Identity, scale=w1, bias=cb[:, ch:ch + 1]),
                                 reads=[ur, "cw", "cb"], writes=[nm])
                            P.op("scalar", lambda e: e.activation(out=edge[:, gi, tt:tt + 1], in_=U[:, 511:512], func=AF.Copy),
                                 reads=[ur], writes=["edge%d_%d" % (gi, tt)])

                        def f_left(U=U, Cb=Cb, w0=w0, t0=t0, ur=ur, nm=nm, gi=gi, tt=tt):
                            P.op("vector", lambda e: e.scalar_tensor_tensor(out=Cb[:, t0 + 1:t0 + 512], in0=U[:, 0:511], scalar=w0, in1=Cb[:, t0 + 1:t0 + 512],
                                                                          op0=ALU.mult, op1=ALU.add),
                                 reads=[ur, nm, "cw", "edge%d_%d" % (gi, tt)], writes=[nm])

                        def f_bnd(Cb=Cb, w0=w0, t0=t0, nm=nm, gi=gi, tt=tt):
                            if tt > 0:
                                P.op("vector", lambda e: e.scalar_tensor_tensor(out=Cb[:, t0:t0 + 1], in0=edge[:, gi, tt - 1:tt], scalar=w0, in1=Cb[:, t0:t0 + 1],
                                                                              op0=ALU.mult, op1=ALU.add),
                                     reads=["edge%d_%d" % (gi, tt - 1), nm, "cw"], writes=[nm])

                        def f_right(U=U, Cb=Cb, w2=w2, t0=t0, lo=lo, ur=ur, nmx=nmx, gi=gi, tt=tt):
                            P.op("vector", lambda e: e.scalar_tensor_tensor(out=Cb[:, t0 + lo - 1:t0 + 511], in0=U[:, lo:512], scalar=w2, in1=Cb[:, t0 + lo - 1:t0 + 511],
                                                                          op0=ALU.mult, op1=ALU.add),
                                 reads=[ur, "cw", "edge%d_%d" % (gi, tt)] + nmx, writes=nmx)
                        stages["act"].append(f_act)
                        stages["left"].append(f_left)
                        stages["right"].append(f_right)
                        stages["bnd"].append(f_bnd)
                    for st_ in ("act", "left", "right", "bnd"):
                        for f_ in stages[st_]:
                            f_()
                for hf in range(2):
                    hs = slice(hf * 2048, (hf + 1) * 2048)
                    P.op("scalar", lambda e, cs=cs, hs=hs: e.activation(out=Cg[cs][:, hs], in_=Cg[cs][:, hs], func=AF.Gelu_apprx_tanh),
                         reads=["Cg%d_h%d" % (cs, hf)], writes=["Cg%d_h%d" % (cs, hf)])
                    P.op("gpsimd", lambda e, cs=cs, hs=hs: e.tensor_tensor(out=gout[cs][:, hs], in0=Cg[cs][:, hs], in1=Cu[cs][:, hs], op=ALU.mult),
                         reads=["Cg%d_h%d" % (cs, hf), "Cu%d_h%d" % (cs, hf)], writes=["gout%d_%d" % (cs, hf)])
                P.dma("sync", "gst%d" % cs, lambda e, cs=cs, fc=fc: e.dma_start(out=gT_d[fc], in_=gout[cs]), reads=["gout%d_0" % cs, "gout%d_1" % cs])
            P.barrier()
            if debug:
                P.dma("sync", "dbg", lambda e: e.dma_start(out=dbg["gT"], in_=gT_d))
                P.barrier()

        if upto >= 5:
            ar.reset()
            wd = ar.bf16_at(WD_OFF, [128, 24, D])
            gTt = [ar.bf16([128, 24, 512]) for _ in range(2)]
            gfpost = ar.f32([128, D])
            x1t = [ar.f32([128, D]) for _ in range(2)]
            yt = [ar.f32([128, D]) for _ in range(2)]
            junk5 = ar.bf16([128, D])
            for i in range(0, WD_C0 // 4):
                P.dma("gpsimd", "wd", lambda e, i=i: e.dma_start(out=wd[:, 4 * i:4 * i + 4, :], in_=w_down_r[:, 4 * i:4 * i + 4, :]), writes=["wd%d" % i])
            P.dma("sync", "gfpost", lambda e: e.dma_start(out=gfpost, in_=n_fpost.partition_broadcast(128)), writes=["gfpost"])
            gT_r = gT_d.rearrange("f p s -> p f s")
            outb5 = [ps[:, 0:2, :], ps[:, 2:4, :]]
            osub = 0
            for tt in range(8):
                t0 = tt * 512
                sl = tt % 2
                if tt == 0:
                    P.dma("sync", "gTt0", lambda e: e.dma_start(out=gTt[0], in_=gT_r[:, :, 0:512]), writes=["gTt0"])
                if tt + 1 < 8:
                    P.dma("sync", "gTt%d" % (1 - sl), lambda e, sl=sl, t0=t0: e.dma_start(out=gTt[1 - sl], in_=gT_r[:, :, t0 + 512:t0 + 1024]), writes=["gTt%d" % (1 - sl)])
                for sub in range(4):
                    ob = osub % 2
                    osub += 1
                    r0 = t0 + sub * 128
                    P.dma("sync", "x1t%d" % ob, lambda e, ob=ob, r0=r0: e.dma_start(out=x1t[ob], in_=x1_d[r0:r0 + 128, :]), writes=["x1t%d" % ob])

                    def mmd(e, ob=ob, sl=sl, sub=sub):
                        last = None
                        for n in range(2):
                            for c in range(24):
                                last = e.matmul(outb5[ob][:, n, :], lhsT=gTt[sl][:, c, sub * 128:(sub + 1) * 128], rhs=wd[:, c, n * 512:(n + 1) * 512],
                                                start=(c == 0), stop=(c == 23))
                        return last
                    P.op("tensor", mmd, reads=["gTt%d" % sl] + ["wd%d" % i for i in range(6)], writes=["outb5%d" % ob])
                    ssq = stat[:, 24 + ob:25 + ob]
                    rs = stat[:, 28 + ob:29 + ob]
                    P.op("scalar", lambda e, ob=ob, ssq=ssq: e.activation(out=junk5.rearrange("p (a b) -> p a b", a=2), in_=outb5[ob], func=AF.Square, accum_out=ssq),
                         reads=["outb5%d" % ob], writes=["junk5", "p5ssq%d" % ob])
                    P.op("gpsimd", lambda e, ssq=ssq: e.tensor_scalar(out=ssq, in0=ssq, scalar1=1.0 / D, scalar2=EPS, op0=ALU.mult, op1=ALU.add),
                         reads=["p5ssq%d" % ob], writes=["p5ms%d" % ob])
                    P.op("gpsimd", lambda e, ssq=ssq, rs=rs: e.tensor_tensor(out=rs, in0=ssq, in1=neghalf[:, 0:1], op=ALU.pow),
                         reads=["p5ms%d" % ob, "neghalf"], writes=["p5rs%d" % ob])
                    P.op("vector", lambda e, ob=ob, rs=rs: e.scalar_tensor_tensor(out=yt[ob].rearrange("p (a b) -> p a b", a=2), in0=outb5[ob], scalar=rs,
                                                                               in1=gfpost.rearrange("p (a b) -> p a b", a=2), op0=ALU.mult, op1=ALU.mult),
                         reads=["outb5%d" % ob, "p5rs%d" % ob, "gfpost"], writes=["yt%d" % ob])
                    P.op("gpsimd", lambda e, ob=ob: e.tensor_tensor(out=yt[ob], in0=yt[ob], in1=x1t[ob], op=ALU.add),
                         reads=["yt%d" % ob, "x1t%d" % ob], writes=["yt%d" % ob])
                    P.dma("sync", "yst%d" % ob, lambda e, ob=ob, r0=r0: e.dma_start(out=y[r0:r0 + 128, :], in_=yt[ob]), reads=["yt%d" % ob])
        P.barrier(["sync"])

        with nc.Block() as block:
            @block.sync
            def _(e):
                for f in P.streams["sync"]:
                    f(e)

            @block.scalar
            def _(e):
                for f in P.streams["scalar"]:
                    f(e)

            @block.vector
            def _(e):
                for f in P.streams["vector"]:
                    f(e)

            @block.gpsimd
            def _(e):
                for f in P.streams["gpsimd"]:
                    f(e)

            @block.tensor
            def _(e):
                for f in P.streams["tensor"]:
                    f(e)
    return nc


def make_consts():
    ident = np.eye(128, dtype=np.float32)
    k = np.arange(128)[:, None]
    q = np.arange(128)[None, :]
    GE = (q >= k)
    LE = (q <= k)
    MF = (k < 64) & (q <= k + 64)
    ML = (k >= 64) & (q >= k - 64)
    masks = np.concatenate([GE, LE, MF, ML], axis=1).astype(np.float32)
    half = 32
    inv = (np.float32(10000.0) ** (-np.arange(half, dtype=np.float32) / np.float32(half))).astype(np.float32)
    ang = (np.arange(S, dtype=np.float32)[:, None] * inv[None, :]).astype(np.float32)
    cos = np.cos(ang).astype(np.float32).T
    sin = np.sin(ang).astype(np.float32).T
    p = np.arange(128)
    cosT = cos[p % 32]
    sgn = np.where((p % 64) < 32, -1.0, 1.0).astype(np.float32)[:, None]
    sinT = sin[p % 32] * sgn
    m = np.arange(128)
    pm = np.where((m % 64) < 32, m + 32, m - 32)
    perm = np.zeros((128, 128), np.float32)
    perm[pm, m] = 1.0
    return {"c_perm": perm, "c_ident": ident, "c_masks": masks, "c_cos": np.ascontiguousarray(cosT), "c_sin": np.ascontiguousarray(sinT)}


_CACHE = {}


def kernel(x, norm_mix_pre, w_in, sink, w_branch_a, w_branch_b, w_gate, b_gate, w_out,
           norm_mix_post, norm_ffn_pre, w_up, conv_w, conv_b, w_down, norm_ffn_post):
    if "nc" not in _CACHE:
        _CACHE["nc"] = build()
    nc = _CACHE["nc"]
    f = lambda a: np.ascontiguousarray(np.asarray(a, dtype=np.float32))
    shared = {
        "w_in": f(w_in[0]), "w_gate": f(w_gate[0]), "b_gate": f(b_gate[0]), "w_out": f(w_out[0]),
        "w_branch_a": f(w_branch_a[0]), "w_branch_b": f(w_branch_b[0]), "w_up": f(w_up[0]),
        "conv_w": f(conv_w[0]), "conv_b": f(conv_b[0]), "w_down": f(w_down[0]),
        "norm_mix_pre": f(norm_mix_pre[0]), "norm_mix_post": f(norm_mix_post[0]),
        "norm_ffn_pre": f(norm_ffn_pre[0]), "norm_ffn_post": f(norm_ffn_post[0]), "sink": f(sink[0]),
    }
    shared.update(make_consts())
    xs = f(x)
    in_maps = [dict(shared, x=xs[b]) for b in range(8)]
    res = run_bass_kernel_spmd(nc, in_maps, core_ids=list(range(8)))
    return np.stack([r["y"] for r in res.results], axis=0).astype(np.float32)
```

```python
import contextlib
import numpy as np
import concourse.bass as bass
import concourse.mybir as mybir
from concourse.bass_utils import run_bass_kernel_spmd

F32 = mybir.dt.float32
BF16 = mybir.dt.bfloat16
ALU = mybir.AluOpType
AF = mybir.ActivationFunctionType

S = 4096
D = 1024
DFF = 3072
EPS = 1e-6
ENG = ["sync", "scalar", "vector", "gpsimd", "tensor"]
ARENA_F32 = 48384


class Prog:
    def __init__(self, nc, stack):
        self.nc = nc
        self.stack = stack
        self.sems = []
        self.streams = {e: [] for e in ENG}
        self.esem = {}
        self.ecount = {}
        for e in ENG:
            if e != "sync":
                self.esem[e] = self.newsem(e)
                self.ecount[e] = 0
        self.res = {}
        self.waited = {e: {} for e in ENG}
        self.dsem = {}
        self.cur = {}
        self.dma_sems = set()

    def newsem(self, name):
        h = self.stack.enter_context(self.nc.semaphore(f"s{len(self.sems)}_{name}"))
        self.sems.append(h)
        return len(self.sems) - 1

    def _deps(self, eng, reads, writes):
        w = []
        for r in reads:
            st = self.res.get(r)
            if st is not None and st["w"] is not None:
                w.append(st["w"])
        for x in writes:
            st = self.res.get(x)
            if st is not None:
                if st["w"] is not None:
                    w.append(st["w"])
                w.extend(st["r"].items())
        best = {}
        for si, val in w:
            if si in self.dma_sems:
                val = self.cur[si]
            if best.get(si, 0) < val:
                best[si] = val
        out = []
        wd = self.waited[eng]
        for si, val in best.items():
            if wd.get(si, 0) < val:
                wd[si] = val
                out.append((si, val))
        return out

    def _update(self, t, reads, writes):
        for r in reads:
            st = self.res.setdefault(r, {"w": None, "r": {}})
            if st["r"].get(t[0], 0) < t[1]:
                st["r"][t[0]] = t[1]
        for x in writes:
            self.res[x] = {"w": t, "r": {}}

    def op(self, eng, fn, reads=(), writes=()):
        waits = self._deps(eng, reads, writes)
        if self.ecount[eng] >= 30000:
            self.esem[eng] = self.newsem(eng)
            self.ecount[eng] = 0
        self.ecount[eng] += 1
        si, val = self.esem[eng], self.ecount[eng]
        self.cur[si] = val
        sems = self.sems

        def emit(e):
            for wi, wv in waits:
                e.wait_ge(sems[wi], wv)
            fn(e).then_inc(sems[si], 1)

        self.streams[eng].append(emit)
        t = (si, val)
        self._update(t, reads, writes)
        return t

    def dma(self, eng, key, fn, reads=(), writes=()):
        waits = self._deps(eng, reads, writes)
        ent = self.dsem.get(key)
        if ent is None or ent[1] >= 30000:
            ent = [self.newsem("d"), 0]
            self.dsem[key] = ent
            self.dma_sems.add(ent[0])
        ent[1] += 16
        si, val = ent[0], ent[1]
        self.cur[si] = val
        sems = self.sems

        def emit(e):
            for wi, wv in waits:
                e.wait_ge(sems[wi], wv)
            fn(e).then_inc(sems[si], 16)

        self.streams[eng].append(emit)
        t = (si, val)
        self._update(t, reads, writes)
        return t

    def barrier(self, engines=ENG):
        sems = self.sems
        for eng in engines:
            ws = []
            wd = self.waited[eng]
            for si, val in self.cur.items():
                if wd.get(si, 0) < val:
                    wd[si] = val
                    ws.append((si, val))

            def emit(e, ws=ws):
                for wi, wv in ws:
                    e.wait_ge(sems[wi], wv)

            self.streams[eng].append(emit)
        self.res = {}


class Arena:
    def __init__(self, ap):
        self.ap = ap
        self.off = 0

    def reset(self, off=0):
        self.off = off

    def _shape(self, a, shape):
        if len(shape) == 2:
            return a
        if len(shape) == 3:
            return a.rearrange("p (a b) -> p a b", a=shape[1])
        if len(shape) == 4:
            return a.rearrange("p (a b c) -> p a b c", a=shape[1], b=shape[2])
        raise ValueError(shape)

    def f32(self, shape):
        n = int(np.prod(shape[1:]))
        assert self.off + n <= ARENA_F32, (self.off, n)
        a = self.ap[:, self.off:self.off + n]
        self.off += n
        return self._shape(a, shape)

    def bf16_at(self, off, shape):
        save = self.off
        self.off = off
        a = self.bf16(shape)
        self.off = save
        return a

    def bf16(self, shape):
        n = int(np.prod(shape[1:]))
        nf = (n + 1) // 2
        assert self.off + nf <= ARENA_F32, (self.off, nf)
        a = self.ap[:, self.off:self.off + nf].bitcast(BF16)[:, 0:n]
        self.off += nf
        return self._shape(a, shape)


def build(upto=99, debug=False):
    nc = bass.Bass("TRN2", target_bir_lowering=False)
    dt = nc.dram_tensor

    def inp(name, shape):
        return dt(name, shape, F32, kind="ExternalInput").ap()

    x = inp("x", [S, D])
    w_in = inp("w_in", [D, 3072])
    w_gate = inp("w_gate", [D, 2048])
    b_gate = inp("b_gate", [2048])
    w_out = inp("w_out", [D, D])
    w_a = inp("w_branch_a", [512, D])
    w_b = inp("w_branch_b", [256, D])
    w_up = inp("w_up", [D, 2 * DFF])
    conv_w = inp("conv_w", [3, 2 * DFF])
    conv_b = inp("conv_b", [2 * DFF])
    w_down = inp("w_down", [DFF, D])
    n_pre = inp("norm_mix_pre", [D])
    n_post = inp("norm_mix_post", [D])
    n_fpre = inp("norm_ffn_pre", [D])
    n_fpost = inp("norm_ffn_post", [D])
    sink = inp("sink", [8])
    c_ident = inp("c_ident", [128, 128])
    c_masks = inp("c_masks", [128, 4 * 128])
    c_cos = inp("c_cos", [128, S])
    c_sin = inp("c_sin", [128, S])
    c_perm = inp("c_perm", [128, 128])
    y = dt("y", [S, D], F32, kind="ExternalOutput").ap()

    qk_d = dt("qk_d", [18, 128, S], BF16, kind="Internal").ap()
    v_d = dt("v_d", [S, 896], BF16, kind="Internal").ap()
    hT_d = dt("hT_d", [128, 8, S], BF16, kind="Internal").ap()
    x1_d = dt("x1_d", [S, D], F32, kind="Internal").ap()
    gT_d = dt("gT_d", [24, 128, S], BF16, kind="Internal").ap()
    dbg = {}
    if debug:
        dbg["qk"] = dt("dbg_qk", [18, 128, S], BF16, kind="ExternalOutput").ap()
        dbg["v"] = dt("dbg_v", [S, 896], BF16, kind="ExternalOutput").ap()
        dbg["yT"] = dt("dbg_yT", [128, 6, S], BF16, kind="ExternalOutput").ap()
        dbg["x1"] = dt("dbg_x1", [S, D], F32, kind="ExternalOutput").ap()
        dbg["gT"] = dt("dbg_gT", [24, 128, S], BF16, kind="ExternalOutput").ap()

    w_in_r = w_in.rearrange("(kc p) n -> p kc n", p=128)
    w_gate_r = w_gate.rearrange("(kc p) n -> p kc n", p=128)
    w_out_r = w_out.rearrange("(kc p) n -> p kc n", p=128)
    w_a_r = w_a.rearrange("(kc p) n -> p kc n", p=128)
    w_b_r = w_b.rearrange("(kc p) n -> p kc n", p=128)
    w_up_r = w_up.rearrange("(kc p) n -> p kc n", p=128)
    w_down_r = w_down.rearrange("(kc p) n -> p kc n", p=128)

    with contextlib.ExitStack() as stack:
        arena_t = stack.enter_context(nc.sbuf_tensor("arena", [128, ARENA_F32], F32))
        small_t = stack.enter_context(nc.sbuf_tensor("small", [128, 768], F32))
        ps_t = stack.enter_context(nc.psum_tensor("ps", [128, 8, 512], F32))
        P = Prog(nc, stack)
        ar = Arena(arena_t[:])
        ps = ps_t[:]

        sm = small_t[:]
        ident = sm[:, 0:64].bitcast(BF16)
        masks = sm[:, 64:320].bitcast(BF16).rearrange("p (a b) -> p a b", a=4)
        onesb = sm[:, 320:384].bitcast(BF16)
        esink = sm[:, 384:388]
        neghalf = sm[:, 388:392]
        stat = sm[:, 392:456]
        bgt = sm[:, 456:472]
        permT = sm[:, 512:576].bitcast(BF16)
        cw = sm[:, 576:720].rearrange("p (a b) -> p a b", a=3)
        cb = sm[:, 720:768]

        P.dma("gpsimd", "c_ident", lambda e: e.dma_start(out=ident, in_=c_ident), writes=["ident"])
        P.dma("gpsimd", "c_masks", lambda e: e.dma_start(out=sm[:, 64:320].bitcast(BF16), in_=c_masks), writes=["masks"])
        P.dma("gpsimd", "c_perm", lambda e: e.dma_start(out=permT, in_=c_perm), writes=["permT"])
        P.op("vector", lambda e: e.memset(onesb, 1.0), writes=["ones"])
        P.op("vector", lambda e: e.memset(neghalf, -0.5), writes=["neghalf"])
        sink2 = sink.rearrange("(t two) -> two t", two=2)
        P.dma("sync", "sink", lambda e: e.dma_start(out=esink[0:64, :], in_=sink2[0, :].partition_broadcast(64), allow_slow_non_contiguous=True), writes=["esink"])
        P.dma("sync", "sink", lambda e: e.dma_start(out=esink[64:128, :], in_=sink2[1, :].partition_broadcast(64), allow_slow_non_contiguous=True), writes=["esink2"])
        P.op("scalar", lambda e: e.activation(out=esink, in_=esink, func=AF.Exp), reads=["esink", "esink2"], writes=["esinkx"])
        esT = sm[:, 472:480]
        P.dma("sync", "esT", lambda e: e.dma_start(out=esT, in_=sink.partition_broadcast(128)), writes=["esT0"])
        P.op("scalar", lambda e: e.activation(out=esT, in_=esT, func=AF.Exp), reads=["esT0"], writes=["esT"])

        def rstd_ops(ssq, rstd, n, tag):
            P.op("gpsimd", lambda e: e.tensor_scalar(out=ssq, in0=ssq, scalar1=1.0 / D, scalar2=EPS, op0=ALU.mult, op1=ALU.add),
                 reads=[tag + "ssq"], writes=[tag + "ssq"])
            P.op("gpsimd", lambda e: e.tensor_tensor(out=rstd, in0=ssq, in1=neghalf[:, 0:n], op=ALU.pow),
                 reads=[tag + "ssq", "neghalf"], writes=[tag + "rstd"])

        def norm_transpose(src_rows, gain_b, n_half, xin, hb, junk, tpbanks, dst_fn, tag, part="both"):
            ns = list(range(n_half)) if isinstance(n_half, int) else list(n_half)
            for n in ns:
                sl = n % len(xin)
                xi = xin[sl]
                if part == "tr":
                    break
                P.dma("sync", tag + "xin%d" % sl, lambda e, n=n, xi=xi: e.dma_start(out=xi, in_=src_rows(n)), writes=[tag + "xin%d" % sl])
                ssq = stat[:, 32 + 4 * sl:32 + 4 * sl + 2]
                rstd = stat[:, 48 + 4 * sl:48 + 4 * sl + 2]
                for s in range(2):
                    P.op("scalar", lambda e, s=s, xi=xi, ssq=ssq: e.activation(out=junk, in_=xi[:, s, :], func=AF.Square, accum_out=ssq[:, s:s + 1]),
                         reads=[tag + "xin%d" % sl], writes=[tag + "junk", tag + "ssq%d_%d" % (sl, s), tag + "ms%d" % sl])
                P.op("gpsimd", lambda e, ssq=ssq: e.tensor_scalar(out=ssq, in0=ssq, scalar1=1.0 / D, scalar2=EPS, op0=ALU.mult, op1=ALU.add),
                     reads=[tag + "ssq%d_0" % sl, tag + "ssq%d_1" % sl], writes=[tag + "ms%d" % sl])
                P.op("gpsimd", lambda e, ssq=ssq, rstd=rstd: e.tensor_tensor(out=rstd, in0=ssq, in1=neghalf[:, 0:2], op=ALU.pow),
                     reads=[tag + "ms%d" % sl, "neghalf"], writes=[tag + "rstd%d" % sl])
                for s in range(2):
                    hbs = hb[sl][:, s, :]
                    P.op("vector", lambda e, s=s, xi=xi, rstd=rstd, hbs=hbs: e.scalar_tensor_tensor(
                        out=hbs, in0=xi[:, s, :], scalar=rstd[:, s:s + 1], in1=gain_b, op0=ALU.mult, op1=ALU.mult),
                        reads=[tag + "xin%d" % sl, tag + "rstd%d" % sl, tag + "gain"], writes=[tag + "hb%d_%d" % (sl, s)])
                if part == "both":
                    norm_tr_part(n, sl, hb, tpbanks, dst_fn, tag)
            if part == "tr":
                for n in ns:
                    norm_tr_part(n, n % len(xin), hb, tpbanks, dst_fn, tag)

        def norm_tr_part(n, sl, hb, tpbanks, dst_fn, tag):
            if True:
                for s in range(2):
                    bk = (2 * n + s) % 2
                    tp = tpbanks[bk]
                    hbs = hb[sl][:, s, :]

                    def tr(e, tp=tp, hbs=hbs):
                        last = None
                        for kc in range(8):
                            last = e.transpose(out=tp[:, kc, :], in_=hbs[:, kc * 128:(kc + 1) * 128], identity=ident)
                        return last
                    P.op("tensor", tr, reads=[tag + "hb%d_%d" % (sl, s), "ident"], writes=[tag + "tp%d" % bk])
                    dst, dres = dst_fn(n, s)
                    eng = "scalar" if s == 0 else "vector"
                    if eng == "scalar":
                        P.op("scalar", lambda e, tp=tp, dst=dst: e.activation(out=dst, in_=tp, func=AF.Copy),
                             reads=[tag + "tp%d" % bk], writes=[dres])
                    else:
                        P.op("vector", lambda e, tp=tp, dst=dst: e.tensor_copy(out=dst, in_=tp),
                             reads=[tag + "tp%d" % bk], writes=[dres])

        tpb = [ps[:, 6, :].bitcast(BF16).rearrange("p (a b) -> p a b", a=8),
               ps[:, 7, :].bitcast(BF16).rearrange("p (a b) -> p a b", a=8)]

        ar.reset()
        wqk = ar.bf16([128, 8, 18, 128])
        qbf = [ar.bf16([128, 512]) for _ in range(2)]
        wv = ar.bf16([128, 8, 896])
        cosT = ar.f32([128, S])
        sinT = ar.f32([128, S])
        gpre = ar.f32([128, D])
        xin = [ar.f32([128, 2, D]) for _ in range(2)]
        junk = ar.bf16([128, D])
        hb = [ar.bf16([128, 2, D]) for _ in range(2)]
        hT = [ar.bf16([128, 8, 512]) for _ in range(2)]
        ra = [ar.f32([128, 512]) for _ in range(2)]
        rb = [ar.f32([128, 512]) for _ in range(2)]
        ro = [ar.bf16([128, 512]) for _ in range(3)]
        vo = [ar.bf16([128, 2, 448]) for _ in range(2)]

        P.dma("sync", "gpre", lambda e: e.dma_start(out=gpre, in_=n_pre.partition_broadcast(128)), writes=["p1gain"])
        col0 = [128 * t for t in range(4)] + [512, 576] + [768 + 128 * t for t in range(6)] + [1536 + 128 * t for t in range(6)]
        for t in range(18):
            c0 = col0[t]
            if t in (4, 5):
                P.dma("gpsimd", "wqk%d" % t, lambda e, t=t, c0=c0: e.dma_start(out=wqk[:, :, t, 0:64], in_=w_in_r[:, :, c0:c0 + 64]), writes=["wqk%da" % t])
                P.dma("gpsimd", "wqk%d" % t, lambda e, t=t, c0=c0: e.dma_start(out=wqk[:, :, t, 64:128], in_=w_in_r[:, :, c0:c0 + 64]), writes=["wqk%db" % t])
            else:
                P.dma("gpsimd", "wqk%d" % t, lambda e, t=t, c0=c0: e.dma_start(out=wqk[:, :, t, :], in_=w_in_r[:, :, c0:c0 + 128]), writes=["wqk%da" % t])
            if t == 0:
                P.dma("gpsimd", "cos", lambda e: e.dma_start(out=cosT, in_=c_cos), writes=["cos"])
                P.dma("gpsimd", "sin", lambda e: e.dma_start(out=sinT, in_=c_sin), writes=["sin"])
            if t == 3:
                P.dma("gpsimd", "wv", lambda e: e.dma_start(out=wv[:, :, 0:128], in_=w_in_r[:, :, 640:768]), writes=["wva"])
                P.dma("gpsimd", "wv", lambda e: e.dma_start(out=wv[:, :, 128:896], in_=w_in_r[:, :, 2304:3072]), writes=["wvb"])
        identF = ar.f32([128, 128])
        rowsA = ar.f32([128, 128])
        rowsB = ar.f32([80, 128]) if False else ar.f32([128, 128])
        P.dma("sync", "identF", lambda e: e.dma_start(out=identF, in_=c_ident), writes=["identF"])
        cw_rows = conv_w.rearrange("t (c p) -> (t c) p", p=128)
        P.dma("sync", "rowsA", lambda e: e.dma_start(out=rowsA, in_=cw_rows[0:128, :]), writes=["rowsA"])
        P.op("vector", lambda e: e.memset(rowsB, 0.0), writes=["rowsB0"])
        P.dma("sync", "rowsB", lambda e: e.dma_start(out=rowsB[0:16, :], in_=cw_rows[128:144, :]), reads=["rowsB0"], writes=["rowsB1"])
        P.dma("sync", "rowsB", lambda e: e.dma_start(out=rowsB[16:64, :], in_=conv_b.rearrange("(c p) -> c p", p=128)), reads=["rowsB0"], writes=["rowsB2"])
        P.dma("sync", "rowsB", lambda e: e.dma_start(out=rowsB[64:80, :], in_=b_gate.rearrange("(c p) -> c p", p=128)), reads=["rowsB0"], writes=["rowsB3"])
        P.op("tensor", lambda e: e.transpose(out=ps[:, 4, 0:128], in_=rowsA, identity=identF), reads=["rowsA", "identF"], writes=["vps"])
        P.op("vector", lambda e: e.tensor_copy(out=sm[:, 576:704], in_=ps[:, 4, 0:128]), reads=["vps"], writes=["cwA"])
        P.op("tensor", lambda e: e.transpose(out=ps[:, 5, 0:128], in_=rowsB, identity=identF), reads=["rowsB0", "rowsB1", "rowsB2", "rowsB3", "identF"], writes=["vps"])
        P.op("vector", lambda e: e.tensor_copy(out=sm[:, 704:768], in_=ps[:, 5, 0:64]), reads=["vps"], writes=["cwB"])
        P.op("vector", lambda e: e.tensor_copy(out=bgt, in_=ps[:, 5, 64:80]), reads=["vps"], writes=["bgt"])

        x_r = x.rearrange("(n s p) d -> n p s d", p=128, s=2)
        qps = [[ps[:, 0, :], ps[:, 1, :]], [ps[:, 2, :], ps[:, 3, :]]]
        vps = ps[:, 4:6, 0:448]

        for tt in range(8):
            t0 = tt * 512
            hts = hT[tt % 2]

            def dst_fn(n, s, hts=hts, tt=tt):
                sub = (n % 2) * 2 + s
                return hts[:, :, sub * 128:(sub + 1) * 128], "p1hT%d_%d" % (tt % 2, sub)
            if tt == 0:
                norm_transpose(lambda n: x_r[n], gpre, 2, xin, hb, junk, tpb, None, "p1", part="norm")
            norm_transpose(None, gpre, 2, xin, hb, junk, tpb, (lambda n, s, tt=tt, f=dst_fn: f(n, s)), "p1", part="tr")
            if tt + 1 < 8:
                norm_transpose(lambda n, tt=tt: x_r[(tt + 1) * 2 + n], gpre, 2, xin, hb, junk, tpb, None, "p1", part="norm")
            hres = ["p1hT%d_%d" % (tt % 2, sub) for sub in range(4)]
            P.dma("sync", "hTst%d" % (tt % 2), lambda e, hts=hts, t0=t0: e.dma_start(out=hT_d[:, :, t0:t0 + 512], in_=hts), reads=hres)
            def straight(t, hts=hts, tt=tt):
                st = (tt * 18 + t) % 2
                bank = qps[st][0]

                def mm(e, t=t, bank=bank, hts=hts):
                    last = None
                    for kc in range(8):
                        last = e.matmul(bank, lhsT=wqk[:, kc, t, :], rhs=hts[:, kc, :], start=(kc == 0), stop=(kc == 7))
                    return last
                wr = ["wqk%da" % t] + (["wqk%db" % t] if t in (4, 5) else [])
                P.op("tensor", mm, reads=hres + wr, writes=["qps%d_0" % st])
                P.op("scalar", lambda e, st=st: e.activation(out=qbf[st], in_=qps[st][0], func=AF.Copy), reads=["qps%d_0" % st], writes=["qbf%d" % st])

            def rotate(t, tt=tt, t0=t0):
                st = (tt * 18 + t) % 2
                P.op("tensor", lambda e, st=st: e.matmul(qps[st][1], lhsT=permT, rhs=qbf[st], start=True, stop=True),
                     reads=["permT", "qbf%d" % st], writes=["qps%d_1" % st])
                P.op("vector", lambda e, st=st, t0=t0: e.tensor_tensor(out=ra[st], in0=qps[st][0], in1=cosT[:, t0:t0 + 512], op=ALU.mult),
                     reads=["qps%d_0" % st, "cos", "qbf%d" % st], writes=["ra%d" % st])
                P.op("vector", lambda e, st=st, t0=t0: e.tensor_tensor(out=rb[st], in0=qps[st][1], in1=sinT[:, t0:t0 + 512], op=ALU.mult),
                     reads=["qps%d_1" % st, "sin"], writes=["rb%d" % st])
                so = (tt * 18 + t) % 3
                P.op("gpsimd", lambda e, st=st, so=so: e.tensor_tensor(out=ro[so], in0=ra[st], in1=rb[st], op=ALU.add),
                     reads=["ra%d" % st, "rb%d" % st], writes=["ro%d" % so])
                P.dma("sync", "rost%d" % so, lambda e, so=so, t=t, t0=t0: e.dma_start(out=qk_d[t, :, t0:t0 + 512], in_=ro[so]), reads=["ro%d" % so])

            for t in range(19):
                if t < 18:
                    straight(t)
                if t >= 1:
                    rotate(t - 1)
            for sub in range(4):
                sv = sub % 2

                vb = 4 + 2 * sv
                vres = ["vps"] if sv == 0 else ["p1tp0", "p1tp1"]

                def mmv(e, sub=sub, hts=hts, vb=vb):
                    last = None
                    for n in range(2):
                        for kc in range(8):
                            last = e.matmul(ps[:, vb + n, 0:448], lhsT=hts[:, kc, sub * 128:(sub + 1) * 128], rhs=wv[:, kc, n * 448:(n + 1) * 448],
                                            start=(kc == 0), stop=(kc == 7))
                    return last
                P.op("tensor", mmv, reads=hres + ["wva", "wvb"], writes=vres)
                P.op("scalar", lambda e, sv=sv, vb=vb: e.activation(out=vo[sv], in_=ps[:, vb:vb + 2, 0:448], func=AF.Copy), reads=vres, writes=["vo%d" % sv])
                P.dma("sync", "vost%d" % sv, lambda e, sv=sv, sub=sub, t0=t0: e.dma_start(
                    out=v_d[t0 + sub * 128:t0 + (sub + 1) * 128, :].rearrange("p (a b) -> p a b", a=2), in_=vo[sv]), reads=["vo%d" % sv])
        P.barrier()
        if debug:
            P.dma("sync", "dbg", lambda e: e.dma_start(out=dbg["qk"], in_=qk_d))
            P.dma("sync", "dbg", lambda e: e.dma_start(out=dbg["v"], in_=v_d))
            P.barrier()

        ar.reset()
        yT = ar.bf16([128, 6, S])
        yT_off = ar.off
        if upto >= 2:
            Q = [ar.bf16([128, S]) for _ in range(2)]
            KZ = [[ar.bf16([128, S]) for _ in range(2)] for _ in range(2)]
            VXs = [ar.bf16([128, 48, 128]) for _ in range(2)]
            VYs = [ar.bf16([128, 48, 128]) for _ in range(2)]
            acc = ar.f32([128, 2, S])
            accX = acc[:, 0, :]
            accY = acc[:, 1, :]
            NR = 5
            Pr = [ar.bf16([128, 2, 3, 128]) for _ in range(NR)]
            rdA = [ar.f32([128, 2]) for _ in range(2)]
            ytok = [ar.bf16([128, 2, 64]) for _ in range(2)]
            dsbw = ar.f32([128, 1024])

            WG_OFF = ARENA_F32 - 8192
            wg = ar.bf16_at(WG_OFF, [128, 8, 2048])
            WG_K0 = -(-(ar.off - WG_OFF) // 1024) if ar.off > WG_OFF else 0
            for kc in range(WG_K0, 8):
                P.dma("gpsimd", "wg", lambda e, kc=kc: e.dma_start(out=wg[:, kc, :], in_=w_gate_r[:, kc, :]), writes=["wg%d" % kc])
            for sl in range(2):
                P.op("gpsimd", lambda e, sl=sl: e.memset(KZ[sl][0][64:128, :], 0.0), writes=["kz%d_0z" % sl])
                P.op("gpsimd", lambda e, sl=sl: e.memset(KZ[sl][1][0:64, :], 0.0), writes=["kz%d_1z" % sl])
            for vs in range(2):
                P.op("vector", lambda e, vs=vs: e.memset(VXs[vs][:, :, 64:128], 1.0), writes=["vx1_%d" % vs])
                P.op("vector", lambda e, vs=vs: e.memset(VYs[vs][:, :, 0:64], 1.0), writes=["vy1_%d" % vs])

            sps = [ps[:, 0:2, :], ps[:, 2:4, :]]
            ops = [ps[:, 4, 0:256].rearrange("p (a b) -> p a b", a=2), ps[:, 5, 0:256].rearrange("p (a b) -> p a b", a=2)]
            opsA = [ps[:, 4, 0:130].rearrange("p (a b) -> p a b", a=2), ps[:, 5, 0:130].rearrange("p (a b) -> p a b", a=2)]

            units = [("A", t, 0, 0) for t in range(4)] + [("B", 0, p, g) for p in range(2) for g in range(3)]
            DIL = [1, 4, 16]
            state = {"pstep": 0, "ostep": 0, "tq": []}

            def load_unit(ui):
                kind, t, p, g = units[ui]
                sl = ui % 2
                if kind == "A":
                    qi, ki, vc = t, 4 + t // 2, (t // 2) * 64
                else:
                    qi, ki, vc = 6 + 2 * g + p, 12 + 2 * g + p, 128 + g * 256 + p * 128
                P.dma("sync", "Q%d" % sl, lambda e: e.dma_start(out=Q[sl], in_=qk_d[qi]), writes=["Q%d" % sl])
                P.dma("sync", "K%d" % sl, lambda e: e.dma_start(out=KZ[sl][0][0:64, :], in_=qk_d[ki, 0:64, :]), writes=["kz%d_0" % sl])
                P.dma("sync", "K%d" % sl, lambda e: e.dma_start(out=KZ[sl][1][64:128, :], in_=qk_d[ki, 64:128, :]), writes=["kz%d_1" % sl])

            def load_v(ui):
                kind, t, p, g = units[ui]
                vs = ui % 2
                VX, VY = VXs[vs], VYs[vs]
                kx, ky = "VX%d" % vs, "VY%d" % vs
                if kind == "A":
                    vc = (t // 2) * 64
                    src = v_d[:, vc:vc + 64].rearrange("(m k) c -> k m c", k=128)
                    for q4 in range(4):
                        P.dma("sync", kx, lambda e, q4=q4: e.dma_start(out=VX[:, 8 * q4:8 * q4 + 8, 0:64], in_=src[:, 8 * q4:8 * q4 + 8, :]), writes=[kx] if q4 == 0 else [])
                        P.dma("sync", ky, lambda e, q4=q4: e.dma_start(out=VY[:, 8 * q4:8 * q4 + 8, 64:128], in_=src[:, 8 * q4:8 * q4 + 8, :]), writes=[ky] if q4 == 0 else [])
                    for key in (kx, ky):
                        ent = P.dsem[key]
                        P.res[key] = {"w": (ent[0], ent[1]), "r": {}}
                    return
                d = DIL[g]
                L = S // d
                M = L // 128
                vc = 128 + g * 256 + p * 128
                first = True
                for r in range(d):
                    cb = r * (M + 1)
                    for (vt, co, key) in ((VX, 0, kx), (VY, 64, ky)):
                        cs = vc + co
                        wr = [key] if first else []
                        o0, i0 = vt[:, cb, co:co + 64], v_d[r:r + d * 127 + 1:d, cs:cs + 64]
                        P.dma("sync", key, lambda e, o0=o0, i0=i0: e.dma_start(out=o0, in_=i0), writes=wr)
                        if M > 1:
                            a = r + d * 64
                            i1f = v_d[a:a + d * (128 * (M - 1) - 1) + 1:d, cs:cs + 64].rearrange("(m k) c -> k m c", k=128)
                            for m0 in range(0, M - 1, 8):
                                m1 = min(m0 + 8, M - 1)
                                o1 = vt[:, cb + 1 + m0:cb + 1 + m1, co:co + 64]
                                i1 = i1f[:, m0:m1, :]
                                P.dma("sync", key, lambda e, o1=o1, i1=i1: e.dma_start(out=o1, in_=i1))
                        a = r + d * (L - 128)
                        o2, i2 = vt[:, cb + M, co:co + 64], v_d[a:a + d * 127 + 1:d, cs:cs + 64]
                        P.dma("sync", key, lambda e, o2=o2, i2=i2: e.dma_start(out=o2, in_=i2))
                    first = False
                for key in (kx, ky):
                    ent = P.dsem[key]
                    P.res[key] = {"w": (ent[0], ent[1]), "r": {}}

            def do_unit(ui):
                kind, t, p, g = units[ui]
                sl = ui % 2
                VX, VY = VXs[sl], VYs[sl]
                if ui + 1 < len(units):
                    load_unit(ui + 1)
                    load_v(ui + 1)
                Qs, Ka, Kb = Q[sl], KZ[sl][0], KZ[sl][1]
                kres = ["Q%d" % sl, "kz%d_0" % sl, "kz%d_1" % sl, "kz%d_0z" % sl, "kz%d_1z" % sl]
                if kind == "A":
                    d, M, nseq = 1, 32, 1
                else:
                    d = DIL[g]
                    M = (S // d) // 128
                    nseq = d
                steps = []
                if kind == "A":
                    steps = [(0, m) for m in range(32)]
                else:
                    steps = [(r, m) for r in range(nseq) for m in range(M + 1)]
                pring = {}
                pending = []
                LAG = 2

                def tok(r, pos0, n):
                    a = r + d * pos0
                    return slice(a, a + d * (n - 1) + 1, d) if d > 1 else slice(a, a + n)

                def emit_S(r, m):
                    pi = state["pstep"] % NR
                    sb = state["pstep"] % 2
                    state["pstep"] += 1
                    pring[(r, m)] = pi
                    if kind == "A":
                        j0, j1 = max(m - 1, 0), min(m + 1, 31)
                        slot0 = j0 - m + 1
                        kpos = m * 128
                    else:
                        j0, j1 = max(m - 1, 0), min(m, M - 1)
                        slot0 = j0 - m + 1
                        L = S // d
                        kpos = 0 if m == 0 else (L - 128 if m == M else 128 * m - 64)
                    nq = j1 - j0 + 1
                    ksl = tok(r, kpos, 128)
                    qsl = tok(r, j0 * 128, nq * 128)
                    c0, c1 = slot0 * 128, (slot0 + nq) * 128
                    for hh in range(2):
                        Kt = Ka if hh == 0 else Kb
                        P.op("tensor", lambda e, hh=hh, Kt=Kt, sb=sb: e.matmul(sps[sb][:, hh, c0:c1], lhsT=Kt[:, ksl], rhs=Qs[:, qsl], start=True, stop=True),
                             reads=kres, writes=["sps%d_%d" % (sb, hh)])
                    Pt = Pr[pi]
                    Pf = Pt.rearrange("p h s q -> p h (s q)")
                    P.op("scalar", lambda e, sb=sb, Pf=Pf: e.activation(out=Pf[:, :, c0:c1], in_=sps[sb][:, :, c0:c1], func=AF.Exp, scale=0.125),
                         reads=["sps%d_0" % sb, "sps%d_1" % sb], writes=["P%d_0" % pi, "P%d_1" % pi])
                    if kind == "A":
                        mlist = []
                        if m > 0:
                            mlist.append((0, 0))
                        if m < 31:
                            mlist.append((2, 1))
                        both = len(mlist) == 2
                        ssl, msl = (slice(0, 3, 2), slice(0, 2)) if both else (slice(mlist[0][0], mlist[0][0] + 1), slice(mlist[0][1], mlist[0][1] + 1))
                    else:
                        if 0 < m < M:
                            ssl, msl = slice(0, 2), slice(0, 2)
                        elif m == 0:
                            ssl, msl = slice(1, 2), slice(2, 3)
                        else:
                            ssl, msl = slice(0, 1), slice(3, 4)
                    def do_masks():
                        for hh, eng in ((0, "gpsimd"), (1, "vector")):
                            P.op(eng, lambda e, Pt=Pt, hh=hh, ssl=ssl, msl=msl: e.tensor_tensor(out=Pt[:, hh, ssl, :], in0=Pt[:, hh, ssl, :], in1=masks[:, msl, :], op=ALU.mult),
                                 reads=["P%d_%d" % (pi, hh), "masks"], writes=["P%d_%d" % (pi, hh)])
                    return do_masks

                def flush_tr():
                    while state["tq"]:
                        jj, obb = state["tq"].pop(0)
                        tb = jj % 2
                        tpa = tpb[tb][:, 0, :]
                        yk2 = ytok[obb]
                        P.op("tensor", lambda e, yk2=yk2, tpa=tpa: e.transpose(out=tpa, in_=yk2.rearrange("p h d -> p (h d)"), identity=ident),
                             reads=["ytok%d" % obb, "ident"], writes=["tpA%d" % tb])
                        P.op("scalar", lambda e, tpa=tpa, jj=jj: e.activation(out=yT[:, t, jj * 128:(jj + 1) * 128], in_=tpa, func=AF.Copy),
                             reads=["tpA%d" % tb], writes=["yTa%d_%d" % (t, jj)])

                def emit_PV(r, j):
                    ob = state["ostep"] % 2
                    state["ostep"] += 1
                    if kind == "A":
                        ms = [m for m in (j - 1, j, j + 1) if 0 <= m <= 31]
                        oa = opsA[ob]
                        for hh in range(2):
                            def pva(e, hh=hh, oa=oa):
                                last = None
                                for i, m in enumerate(ms):
                                    last = e.matmul(oa[:, hh, :], lhsT=Pr[pring[(r, m)]][:, hh, j - m + 1, :], rhs=VX[:, m, 0:65],
                                                    start=(i == 0), stop=(i == len(ms) - 1))
                                return last
                            P.op("tensor", pva, reads=["P%d_%d" % (pring[(r, m)], hh) for m in ms] + ["VX%d" % sl, "vx1_%d" % sl], writes=["ops%d_%d" % (ob, hh)])
                        flush_tr()
                        rd = rdA[ob]
                        yk = ytok[ob]
                        ores = ["ops%d_0" % ob, "ops%d_1" % ob]
                        P.op("vector", lambda e, oa=oa, rd=rd: e.tensor_tensor(out=rd, in0=oa[:, :, 64], in1=esT[:, 2 * t:2 * t + 2], op=ALU.add),
                             reads=ores + ["esT"], writes=["rdA%d" % ob])
                        P.op("vector", lambda e, rd=rd: e.reciprocal(out=rd, in_=rd), reads=["rdA%d" % ob], writes=["rdA%d" % ob])
                        P.op("vector", lambda e, oa=oa, rd=rd, yk=yk: e.tensor_tensor(out=yk, in0=oa[:, :, 0:64], in1=rd.unsqueeze(2).broadcast_to([128, 2, 64]), op=ALU.mult),
                             reads=ores + ["rdA%d" % ob], writes=["ytok%d" % ob])
                        state["tq"].append((j, ob))
                        return
                    ms = [j, j + 1]
                    chunk = lambda m: r * (M + 1) + m
                    for hh in range(2):
                        Vt = VX if hh == 0 else VY

                        def pv(e, hh=hh, Vt=Vt, ob=ob):
                            last = None
                            for i, m in enumerate(ms):
                                last = e.matmul(ops[ob][:, hh, :], lhsT=Vt[:, chunk(m), :], rhs=Pr[pring[(r, m)]][:, hh, j - m + 1, :],
                                                start=(i == 0), stop=(i == len(ms) - 1))
                            return last
                        P.op("tensor", pv, reads=["P%d_%d" % (pring[(r, m)], hh) for m in ms] + ["VX%d" % sl, "VY%d" % sl, "vx1_%d" % sl, "vy1_%d" % sl], writes=["ops%d_%d" % (ob, hh)])
                    o = ops[ob]
                    if True:
                        cols = tok(r, j * 128, 128)
                        if g == 0:
                            P.op("vector", lambda e, o=o: e.tensor_copy(out=acc[:, :, cols], in_=o),
                                 reads=["ops%d_0" % ob, "ops%d_1" % ob], writes=["accX", "accY"])
                        else:
                            P.op("vector", lambda e, o=o: e.tensor_tensor(out=acc[:, :, cols], in0=o, in1=acc[:, :, cols], op=ALU.add),
                                 reads=["ops%d_0" % ob, "ops%d_1" % ob, "accX", "accY"], writes=["accX", "accY"])

                for (r, m) in steps:
                    masks_later = emit_S(r, m)
                    pending = [(rr, jj, c - 1) for (rr, jj, c) in pending]
                    if kind == "A":
                        if m >= 1:
                            pending.append((r, m - 1, LAG))
                        if m == 31:
                            pending.append((r, 31, LAG))
                    else:
                        if m >= 1:
                            pending.append((r, m - 1, LAG))
                    while pending and pending[0][2] <= 0:
                        rr, jj, _ = pending.pop(0)
                        emit_PV(rr, jj)
                    masks_later()
                for (rr, jj, _) in pending:
                    emit_PV(rr, jj)
                pending = []
                flush_tr()

                if kind == "B" and g == 2:
                    for c in range(4):
                        cs = slice(c * 1024, (c + 1) * 1024)
                        P.op("scalar", lambda e, cs=cs: e.activation(out=dsbw[0:64, :], in_=accX[64:128, cs], func=AF.Copy),
                             reads=["accX"], writes=["dsbw_lo"])
                        P.op("scalar", lambda e, cs=cs: e.activation(out=dsbw[64:128, :], in_=accY[0:64, cs], func=AF.Copy),
                             reads=["accY"], writes=["dsbw_hi"])
                        P.op("scalar", lambda e: e.activation(out=dsbw, in_=dsbw, func=AF.Ln), reads=["dsbw_lo", "dsbw_hi"], writes=["dsbw_lo", "dsbw_hi"])
                        P.op("scalar", lambda e: e.activation(out=dsbw, in_=dsbw, func=AF.Exp, scale=-1.0), reads=["dsbw_lo", "dsbw_hi"], writes=["dsbw_lo", "dsbw_hi"])
                        P.op("vector", lambda e, cs=cs: e.tensor_tensor(out=yT[0:64, 4 + p, cs], in0=accX[0:64, cs], in1=dsbw[0:64, :], op=ALU.mult),
                             reads=["accX", "dsbw_lo"], writes=["yTBa%d_%d" % (p, c)])
                        P.op("vector", lambda e, cs=cs: e.tensor_tensor(out=yT[64:128, 4 + p, cs], in0=accY[64:128, cs], in1=dsbw[64:128, :], op=ALU.mult),
                             reads=["accY", "dsbw_hi"], writes=["yTBb%d_%d" % (p, c)])

            load_unit(0)
            load_v(0)
            for ui in range(len(units)):
                do_unit(ui)
            P.barrier()
            if debug:
                P.dma("sync", "dbg", lambda e: e.dma_start(out=dbg["yT"], in_=yT))
                P.barrier()

        if upto >= 3:
            ar.reset(yT_off)
            wg = ar.bf16_at(WG_OFF, [128, 8, 2048])
            wa = ar.bf16([128, 4, D])
            wb = ar.bf16([128, 2, D])
            wo = ar.bf16([128, 8, D])
            hTt = [ar.bf16([128, 8, 512]) for _ in range(2)]
            gpost = ar.f32([128, D])
            sA = [ar.f32([128, 512]) for _ in range(2)]
            sB = [ar.f32([128, 512]) for _ in range(2)]
            m1 = [ar.f32([128, 512]) for _ in range(2)]
            m2 = [ar.f32([128, 512]) for _ in range(2)]
            mT = [ar.bf16([128, 8, 512]) for _ in range(2)]
            xres = [ar.f32([128, D]) for _ in range(2)]
            ytmp = [ar.f32([128, D]) for _ in range(2)]
            junk3 = ar.bf16([128, D])
            assert WG_K0 == 8 and ar.off <= WG_OFF
            for dc in range(8):
                dsl_ = slice(dc * 128, (dc + 1) * 128)
                dsl2 = slice(1024 + dc * 128, 1024 + (dc + 1) * 128)
                P.dma("gpsimd", "wgdc%d" % dc, lambda e, dsl_=dsl_: e.dma_start(out=wg[:, :, dsl_], in_=w_gate_r[:, :, dsl_]), writes=["wgA%d" % dc])
                P.dma("gpsimd", "wgdc%d" % dc, lambda e, dsl2=dsl2: e.dma_start(out=wg[:, :, dsl2], in_=w_gate_r[:, :, dsl2]), writes=["wgB%d" % dc])
                P.dma("gpsimd", "wgdc%d" % dc, lambda e, dsl_=dsl_: e.dma_start(out=wa[:, :, dsl_], in_=w_a_r[:, :, dsl_]), writes=["wa%d" % dc])
                P.dma("gpsimd", "wgdc%d" % dc, lambda e, dsl_=dsl_: e.dma_start(out=wb[:, :, dsl_], in_=w_b_r[:, :, dsl_]), writes=["wb%d" % dc])
            for i in range(2):
                P.dma("gpsimd", "wo", lambda e, i=i: e.dma_start(out=wo[:, :, i * 512:(i + 1) * 512], in_=w_out_r[:, :, i * 512:(i + 1) * 512]), writes=["wo%d" % i])
            P.dma("sync", "gpost", lambda e: e.dma_start(out=gpost, in_=n_post.partition_broadcast(128)), writes=["gpost"])
            mb = [ps[:, i, :] for i in range(4)]
            outb = [ps[:, 4:6, :], ps[:, 6:8, :]]
            osub = 0
            for tt in range(8):
                t0 = tt * 512
                sl = tt % 2
                if tt == 0:
                    P.dma("sync", "hTt0", lambda e: e.dma_start(out=hTt[0], in_=hT_d[:, :, 0:512]), writes=["hTt0"])
                if tt + 1 < 8:
                    P.dma("sync", "hTt%d" % (1 - sl), lambda e, sl=sl, t0=t0: e.dma_start(out=hTt[1 - sl], in_=hT_d[:, :, t0 + 512:t0 + 1024]), writes=["hTt%d" % (1 - sl)])
                for dc in range(8):
                    s2 = dc % 2
                    dsl = slice(dc * 128, (dc + 1) * 128)

                    def mmg(e, off, bank, sl=sl, dc=dc):
                        last = None
                        for kc in range(8):
                            last = e.matmul(bank, lhsT=wg[:, kc, off + dc * 128:off + (dc + 1) * 128], rhs=hTt[sl][:, kc, :], start=(kc == 0), stop=(kc == 7))
                        return last
                    P.op("tensor", lambda e, f=mmg: f(e, 0, mb[0]), reads=["wgA%d" % dc, "hTt%d" % sl], writes=["mb0"])
                    P.op("tensor", lambda e, f=mmg: f(e, 1024, mb[1]), reads=["wgB%d" % dc, "hTt%d" % sl], writes=["mb1"])

                    def mma(e, dsl=dsl, t0=t0):
                        last = None
                        for kc in range(4):
                            last = e.matmul(mb[2], lhsT=wa[:, kc, dsl], rhs=yT[:, kc, t0:t0 + 512], start=(kc == 0), stop=(kc == 3))
                        return last
                    P.op("tensor", mma, reads=["wa%d" % dc], writes=["mb2"])

                    def mmb(e, dsl=dsl, t0=t0):
                        last = None
                        for kc in range(2):
                            last = e.matmul(mb[3], lhsT=wb[:, kc, dsl], rhs=yT[:, 4 + kc, t0:t0 + 512], start=(kc == 0), stop=(kc == 1))
                        return last
                    P.op("tensor", mmb, reads=["wb%d" % dc], writes=["mb3"])
                    P.op("scalar", lambda e, s2=s2, dc=dc: e.activation(out=sA[s2], in_=mb[0], func=AF.Sigmoid, bias=bgt[:, dc:dc + 1], scale=1.0),
                         reads=["mb0", "bgt"], writes=["sA%d" % s2])
                    P.op("scalar", lambda e, s2=s2, dc=dc: e.activation(out=sB[s2], in_=mb[1], func=AF.Sigmoid, bias=bgt[:, 8 + dc:9 + dc], scale=1.0),
                         reads=["mb1", "bgt"], writes=["sB%d" % s2])
                    P.op("vector", lambda e, s2=s2: e.tensor_tensor(out=m1[s2], in0=mb[2], in1=sA[s2], op=ALU.mult), reads=["mb2", "sA%d" % s2], writes=["m1_%d" % s2])
                    P.op("vector", lambda e, s2=s2: e.tensor_tensor(out=m2[s2], in0=mb[3], in1=sB[s2], op=ALU.mult), reads=["mb3", "sB%d" % s2], writes=["m2_%d" % s2])
                    P.op("gpsimd", lambda e, s2=s2, sl=sl, dc=dc: e.tensor_tensor(out=mT[sl][:, dc, :], in0=m1[s2], in1=m2[s2], op=ALU.add),
                         reads=["m1_%d" % s2, "m2_%d" % s2], writes=["mT%d_%d" % (sl, dc)])
                mres = ["mT%d_%d" % (sl, dc) for dc in range(8)]
                for sub in range(4):
                    ob = osub % 2
                    osub += 1
                    r0 = t0 + sub * 128
                    P.dma("sync", "xres%d" % ob, lambda e, ob=ob, r0=r0: e.dma_start(out=xres[ob], in_=x[r0:r0 + 128, :]), writes=["xres%d" % ob])

                    def mmo(e, ob=ob, sl=sl, sub=sub):
                        last = None
                        for n in range(2):
                            for c in range(8):
                                last = e.matmul(outb[ob][:, n, :], lhsT=mT[sl][:, c, sub * 128:(sub + 1) * 128], rhs=wo[:, c, n * 512:(n + 1) * 512],
                                                start=(c == 0), stop=(c == 7))
                        return last
                    P.op("tensor", mmo, reads=mres + ["wo0", "wo1"], writes=["outb%d" % ob])
                    ssq = stat[:, 16 + ob:17 + ob]
                    rs = stat[:, 20 + ob:21 + ob]
                    P.op("scalar", lambda e, ob=ob, ssq=ssq: e.activation(out=junk3.rearrange("p (a b) -> p a b", a=2), in_=outb[ob], func=AF.Square, accum_out=ssq),
                         reads=["outb%d" % ob], writes=["junk3", "p3ssq%d" % ob])
                    P.op("gpsimd", lambda e, ssq=ssq: e.tensor_scalar(out=ssq, in0=ssq, scalar1=1.0 / D, scalar2=EPS, op0=ALU.mult, op1=ALU.add),
                         reads=["p3ssq%d" % ob], writes=["p3ms%d" % ob])
                    P.op("gpsimd", lambda e, ssq=ssq, rs=rs: e.tensor_tensor(out=rs, in0=ssq, in1=neghalf[:, 0:1], op=ALU.pow),
                         reads=["p3ms%d" % ob, "neghalf"], writes=["p3rs%d" % ob])
                    P.op("vector", lambda e, ob=ob, rs=rs: e.scalar_tensor_tensor(out=ytmp[ob].rearrange("p (a b) -> p a b", a=2), in0=outb[ob], scalar=rs,
                                                                               in1=gpost.rearrange("p (a b) -> p a b", a=2), op0=ALU.mult, op1=ALU.mult),
                         reads=["outb%d" % ob, "p3rs%d" % ob, "gpost"], writes=["ytmp%d" % ob])
                    P.op("vector", lambda e, ob=ob: e.tensor_tensor(out=ytmp[ob], in0=ytmp[ob], in1=xres[ob], op=ALU.add),
                         reads=["ytmp%d" % ob, "xres%d" % ob], writes=["ytmp%d" % ob])
                    P.dma("sync", "x1st%d" % ob, lambda e, ob=ob, r0=r0: e.dma_start(out=x1_d[r0:r0 + 128, :], in_=ytmp[ob]), reads=["ytmp%d" % ob])
            P.barrier()
            if debug:
                P.dma("sync", "dbg", lambda e: e.dma_start(out=dbg["x1"], in_=x1_d))
                P.barrier()

        if upto >= 4:
            ar.reset()
            h2T = ar.bf16([128, 8, S])
            h2_off = ar.off
            gfpre = ar.f32([128, D])
            xin4 = [ar.f32([128, 2, D]) for _ in range(4)]
            junk4 = ar.bf16([128, D])
            hb4 = [ar.bf16([128, 2, D]) for _ in range(4)]
            P.dma("sync", "gfpre", lambda e: e.dma_start(out=gfpre, in_=n_fpre.partition_broadcast(128)), writes=["p4gain"])
            x1_r = x1_d.rearrange("(n s p) d -> n p s d", p=128, s=2)

            def dst4(n, s):
                tk = n * 256 + s * 128
                return h2T[:, :, tk:tk + 128], "h2T_%d" % (n * 2 + s)
            norm_transpose(lambda n: x1_r[n], gfpre, [0, 1], xin4, hb4, junk4, tpb, None, "p4", part="norm")
            for n4 in range(16):
                norm_transpose(None, gfpre, [n4], xin4, hb4, junk4, tpb, dst4, "p4", part="tr")
                if n4 + 2 < 16:
                    norm_transpose(lambda n: x1_r[n], gfpre, [n4 + 2], xin4, hb4, junk4, tpb, None, "p4", part="norm")
            P.barrier()

            ar.reset(h2_off)
            wup = [ar.bf16([128, 8, 256]) for _ in range(3)]
            Cg = [ar.f32([128, S]) for _ in range(2)]
            Cu = [ar.f32([128, S]) for _ in range(2)]
            gout = [ar.bf16([128, S]) for _ in range(2)]
            edge = ar.f32([128, 2, 8])
            WD_OFF = ARENA_F32 - 12288
            wd = ar.bf16_at(WD_OFF, [128, 24, D])
            WD_C0 = -(-(ar.off - WD_OFF) // 512) if ar.off > WD_OFF else 0
            WD_C0 = -(-WD_C0 // 4) * 4
            wd_todo = list(range(WD_C0 // 4, 6))

            def load_wup(fc):
                sl = fc % 3
                P.dma("gpsimd", "wup%d" % sl, lambda e: e.dma_start(out=wup[sl][:, :, 0:128], in_=w_up_r[:, :, fc * 128:(fc + 1) * 128]), writes=["wup%d" % sl])
                P.dma("gpsimd", "wup%d" % sl, lambda e: e.dma_start(out=wup[sl][:, :, 128:256], in_=w_up_r[:, :, DFF + fc * 128:DFF + (fc + 1) * 128]), writes=["wup%db" % sl])
            load_wup(0)
            load_wup(1)
            ustep = 0
            for fc in range(24):
                if fc + 2 < 24:
                    load_wup(fc + 2)
                if fc >= 2 and wd_todo:
                    i_ = wd_todo.pop(0)
                    P.dma("gpsimd", "wd", lambda e, i=i_: e.dma_start(out=wd[:, 4 * i:4 * i + 4, :], in_=w_down_r[:, 4 * i:4 * i + 4, :]), writes=["wd%d" % i_])
                sl = fc % 3
                cs = fc % 2
                for tt in range(8):
                    t0 = tt * 512
                    ub = ustep % 4
                    ustep += 1
                    banks = (ps[:, 2 * ub, :], ps[:, 2 * ub + 1, :])
                    for gi in range(2):
                        def mmu(e, gi=gi, bank=banks[gi], sl=sl, t0=t0):
                            last = None
                            for kc in range(8):
                                last = e.matmul(bank, lhsT=wup[sl][:, kc, gi * 128:(gi + 1) * 128], rhs=h2T[:, kc, t0:t0 + 512], start=(kc == 0), stop=(kc == 7))
                            return last
                        P.op("tensor", mmu, reads=["wup%d" % sl, "wup%db" % sl], writes=["ub%d_%d" % (ub, gi)])
                    for gi, Cb, nm0 in ((0, Cg[cs], "Cg%d" % cs), (1, Cu[cs], "Cu%d" % cs)):
                        nm = nm0 + "_h%d" % (tt // 4)
                        nmx = [nm0 + "_h0", nm0 + "_h1"] if tt == 4 else [nm]
                        ch = gi * 24 + fc
                        U = banks[gi]
                        ur = "ub%d_%d" % (ub, gi)
                        w0 = cw[:, 0, ch:ch + 1]
                        w1 = cw[:, 1, ch:ch + 1]
                        w2 = cw[:, 2, ch:ch + 1]
                        P.op("scalar", lambda e, U=U, Cb=Cb, w1=w1, ch=ch, t0=t0: e.activation(out=Cb[:, t0:t0 + 512], in_=U, func=AF.Identity, scale=w1, bias=cb[:, ch:ch + 1]),
                             reads=[ur, "cw", "cb"], writes=[nm])
                        P.op("scalar", lambda e, U=U, gi=gi, tt=tt: e.activation(out=edge[:, gi, tt:tt + 1], in_=U[:, 511:512], func=AF.Copy),
                             reads=[ur], writes=["edge%d_%d" % (gi, tt)])
                        P.op("vector", lambda e, U=U, Cb=Cb, w0=w0, t0=t0: e.scalar_tensor_tensor(out=Cb[:, t0 + 1:t0 + 512], in0=U[:, 0:511], scalar=w0, in1=Cb[:, t0 + 1:t0 + 512],
                                                                                         op0=ALU.mult, op1=ALU.add),
                             reads=[ur, nm, "cw", "edge%d_%d" % (gi, tt)], writes=[nm])
                        if tt > 0:
                            P.op("vector", lambda e, Cb=Cb, w0=w0, t0=t0, gi=gi, tt=tt: e.scalar_tensor_tensor(out=Cb[:, t0:t0 + 1], in0=edge[:, gi, tt - 1:tt], scalar=w0, in1=Cb[:, t0:t0 + 1],
                                                                                                          op0=ALU.mult, op1=ALU.add),
                                 reads=["edge%d_%d" % (gi, tt - 1), nm, "cw"], writes=[nm])
                        lo = 1 if tt == 0 else 0
                        nm_save = nm
                        P.op("vector", lambda e, U=U, Cb=Cb, w2=w2, t0=t0, lo=lo: e.scalar_tensor_tensor(out=Cb[:, t0 + lo - 1:t0 + 511], in0=U[:, lo:512], scalar=w2, in1=Cb[:, t0 + lo - 1:t0 + 511],
                                                                                                 op0=ALU.mult, op1=ALU.add),
                             reads=[ur, "cw"] + nmx, writes=nmx)
                for hf in range(2):
                    hs = slice(hf * 2048, (hf + 1) * 2048)
                    P.op("scalar", lambda e, cs=cs, hs=hs: e.activation(out=Cg[cs][:, hs], in_=Cg[cs][:, hs], func=AF.Gelu_apprx_tanh),
                         reads=["Cg%d_h%d" % (cs, hf)], writes=["Cg%d_h%d" % (cs, hf)])
                    P.op("gpsimd", lambda e, cs=cs, hs=hs: e.tensor_tensor(out=gout[cs][:, hs], in0=Cg[cs][:, hs], in1=Cu[cs][:, hs], op=ALU.mult),
                         reads=["Cg%d_h%d" % (cs, hf), "Cu%d_h%d" % (cs, hf)], writes=["gout%d_%d" % (cs, hf)])
                P.dma("sync", "gst%d" % cs, lambda e, cs=cs, fc=fc: e.dma_start(out=gT_d[fc], in_=gout[cs]), reads=["gout%d_0" % cs, "gout%d_1" % cs])
            P.barrier()
            if debug:
                P.dma("sync", "dbg", lambda e: e.dma_start(out=dbg["gT"], in_=gT_d))
                P.barrier()

        if upto >= 5:
            ar.reset()
            wd = ar.bf16_at(WD_OFF, [128, 24, D])
            gTt = [ar.bf16([128, 24, 512]) for _ in range(2)]
            gfpost = ar.f32([128, D])
            x1t = [ar.f32([128, D]) for _ in range(2)]
            yt = [ar.f32([128, D]) for _ in range(2)]
            junk5 = ar.bf16([128, D])
            for i in range(0, WD_C0 // 4):
                P.dma("gpsimd", "wd", lambda e, i=i: e.dma_start(out=wd[:, 4 * i:4 * i + 4, :], in_=w_down_r[:, 4 * i:4 * i + 4, :]), writes=["wd%d" % i])
            P.dma("sync", "gfpost", lambda e: e.dma_start(out=gfpost, in_=n_fpost.partition_broadcast(128)), writes=["gfpost"])
            gT_r = gT_d.rearrange("f p s -> p f s")
            outb5 = [ps[:, 0:2, :], ps[:, 2:4, :]]
            osub = 0
            for tt in range(8):
                t0 = tt * 512
                sl = tt % 2
                if tt == 0:
                    P.dma("sync", "gTt0", lambda e: e.dma_start(out=gTt[0], in_=gT_r[:, :, 0:512]), writes=["gTt0"])
                if tt + 1 < 8:
                    P.dma("sync", "gTt%d" % (1 - sl), lambda e, sl=sl, t0=t0: e.dma_start(out=gTt[1 - sl], in_=gT_r[:, :, t0 + 512:t0 + 1024]), writes=["gTt%d" % (1 - sl)])
                for sub in range(4):
                    ob = osub % 2
                    osub += 1
                    r0 = t0 + sub * 128
                    P.dma("sync", "x1t%d" % ob, lambda e, ob=ob, r0=r0: e.dma_start(out=x1t[ob], in_=x1_d[r0:r0 + 128, :]), writes=["x1t%d" % ob])

                    def mmd(e, ob=ob, sl=sl, sub=sub):
                        last = None
                        for n in range(2):
                            for c in range(24):
                                last = e.matmul(outb5[ob][:, n, :], lhsT=gTt[sl][:, c, sub * 128:(sub + 1) * 128], rhs=wd[:, c, n * 512:(n + 1) * 512],
                                                start=(c == 0), stop=(c == 23))
                        return last
                    P.op("tensor", mmd, reads=["gTt%d" % sl] + ["wd%d" % i for i in range(6)], writes=["outb5%d" % ob])
                    ssq = stat[:, 24 + ob:25 + ob]
                    rs = stat[:, 28 + ob:29 + ob]
                    P.op("scalar", lambda e, ob=ob, ssq=ssq: e.activation(out=junk5.rearrange("p (a b) -> p a b", a=2), in_=outb5[ob], func=AF.Square, accum_out=ssq),
                         reads=["outb5%d" % ob], writes=["junk5", "p5ssq%d" % ob])
                    P.op("gpsimd", lambda e, ssq=ssq: e.tensor_scalar(out=ssq, in0=ssq, scalar1=1.0 / D, scalar2=EPS, op0=ALU.mult, op1=ALU.add),
                         reads=["p5ssq%d" % ob], writes=["p5ms%d" % ob])
                    P.op("gpsimd", lambda e, ssq=ssq, rs=rs: e.tensor_tensor(out=rs, in0=ssq, in1=neghalf[:, 0:1], op=ALU.pow),
                         reads=["p5ms%d" % ob, "neghalf"], writes=["p5rs%d" % ob])
                    P.op("vector", lambda e, ob=ob, rs=rs: e.scalar_tensor_tensor(out=yt[ob].rearrange("p (a b) -> p a b", a=2), in0=outb5[ob], scalar=rs,
                                                                               in1=gfpost.rearrange("p (a b) -> p a b", a=2), op0=ALU.mult, op1=ALU.mult),
                         reads=["outb5%d" % ob, "p5rs%d" % ob, "gfpost"], writes=["yt%d" % ob])
                    P.op("gpsimd", lambda e, ob=ob: e.tensor_tensor(out=yt[ob], in0=yt[ob], in1=x1t[ob], op=ALU.add),
                         reads=["yt%d" % ob, "x1t%d" % ob], writes=["yt%d" % ob])
                    P.dma("sync", "yst%d" % ob, lambda e, ob=ob, r0=r0: e.dma_start(out=y[r0:r0 + 128, :], in_=yt[ob]), reads=["yt%d" % ob])
        P.barrier(["sync"])

        with nc.Block() as block:
            @block.sync
            def _(e):
                for f in P.streams["sync"]:
                    f(e)

            @block.scalar
            def _(e):
                for f in P.streams["scalar"]:
                    f(e)

            @block.vector
            def _(e):
                for f in P.streams["vector"]:
                    f(e)

            @block.gpsimd
            def _(e):
                for f in P.streams["gpsimd"]:
                    f(e)

            @block.tensor
            def _(e):
                for f in P.streams["tensor"]:
                    f(e)
    return nc


def make_consts():
    ident = np.eye(128, dtype=np.float32)
    k = np.arange(128)[:, None]
    q = np.arange(128)[None, :]
    GE = (q >= k)
    LE = (q <= k)
    MF = (k < 64) & (q <= k + 64)
    ML = (k >= 64) & (q >= k - 64)
    masks = np.concatenate([GE, LE, MF, ML], axis=1).astype(np.float32)
    half = 32
    inv = (np.float32(10000.0) ** (-np.arange(half, dtype=np.float32) / np.float32(half))).astype(np.float32)
    ang = (np.arange(S, dtype=np.float32)[:, None] * inv[None, :]).astype(np.float32)
    cos = np.cos(ang).astype(np.float32).T
    sin = np.sin(ang).astype(np.float32).T
    p = np.arange(128)
    cosT = cos[p % 32]
    sgn = np.where((p % 64) < 32, -1.0, 1.0).astype(np.float32)[:, None]
    sinT = sin[p % 32] * sgn
    m = np.arange(128)
    pm = np.where((m % 64) < 32, m + 32, m - 32)
    perm = np.zeros((128, 128), np.float32)
    perm[pm, m] = 1.0
    return {"c_perm": perm, "c_ident": ident, "c_masks": masks, "c_cos": np.ascontiguousarray(cosT), "c_sin": np.ascontiguousarray(sinT)}


_CACHE = {}


def kernel(x, norm_mix_pre, w_in, sink, w_branch_a, w_branch_b, w_gate, b_gate, w_out,
           norm_mix_post, norm_ffn_pre, w_up, conv_w, conv_b, w_down, norm_ffn_post):
    if "nc" not in _CACHE:
        _CACHE["nc"] = build()
    nc = _CACHE["nc"]
    f = lambda a: np.ascontiguousarray(np.asarray(a, dtype=np.float32))
    shared = {
        "w_in": f(w_in[0]), "w_gate": f(w_gate[0]), "b_gate": f(b_gate[0]), "w_out": f(w_out[0]),
        "w_branch_a": f(w_branch_a[0]), "w_branch_b": f(w_branch_b[0]), "w_up": f(w_up[0]),
        "conv_w": f(conv_w[0]), "conv_b": f(conv_b[0]), "w_down": f(w_down[0]),
        "norm_mix_pre": f(norm_mix_pre[0]), "norm_mix_post": f(norm_mix_post[0]),
        "norm_ffn_pre": f(norm_ffn_pre[0]), "norm_ffn_post": f(norm_ffn_post[0]), "sink": f(sink[0]),
    }
    shared.update(make_consts())
    xs = f(x)
    in_maps = [dict(shared, x=xs[b]) for b in range(8)]
    res = run_bass_kernel_spmd(nc, in_maps, core_ids=list(range(8)))
    return np.stack([r["y"] for r in res.results], axis=0).astype(np.float32)
```
